# Optimizing a Trainium2 kernel written in Bass

```python
import jax, jax.numpy as jnp
from jax import lax
import numpy as np

D_MODEL = 1024
BATCH = 8
SEQ = 4096
DEPTH = 2

CHUNK = 64
N_PREV_CHUNKS = 8
BAND = (N_PREV_CHUNKS + 1) * CHUNK
HEAD_DIM = 64
H_A = 8
H_B = 8
W_A = H_A * HEAD_DIM
W_B = H_B * HEAD_DIM
W_MIX = W_A + W_B
W_IN = 3 * W_MIX
REL_CLIP = 128
SB_BLOCK = 128
D_FF = 2816
CONV_WIDTH = 3
N_MOD = 6
EPS = 1e-6

kernel_name = "hybrid_chunked_stickbreaking_convffn"


def rms_norm(x):
    xf = x.astype(jnp.float32)
    y = xf * lax.rsqrt(jnp.mean(xf * xf, axis=-1, keepdims=True) + EPS)
    return y.astype(x.dtype)


def modulate(h, shift, scale):
    return h * (1 + scale[:, None, :]) + shift[:, None, :]


def rel_bias_band(rel_bias):
    q_pos = N_PREV_CHUNKS * CHUNK + jnp.arange(CHUNK)
    k_pos = jnp.arange(BAND)
    dist = jnp.clip(q_pos[:, None] - k_pos[None, :], -REL_CLIP, REL_CLIP) + REL_CLIP
    return rel_bias[:, dist]


def chunked_rel_attention(q, k, v, rel_bias):
    b, s, h, dh = q.shape
    n_chunks = s // CHUNK
    qc = q.reshape(b, n_chunks, CHUNK, h, dh)
    pad = ((0, 0), (N_PREV_CHUNKS, 0), (0, 0), (0, 0), (0, 0))
    kp = jnp.pad(k.reshape(b, n_chunks, CHUNK, h, dh), pad)
    vp = jnp.pad(v.reshape(b, n_chunks, CHUNK, h, dh), pad)
    band_idx = jnp.arange(n_chunks)[:, None] + jnp.arange(N_PREV_CHUNKS + 1)[None, :]
    kb = kp[:, band_idx].reshape(b, n_chunks, BAND, h, dh)
    vb = vp[:, band_idx].reshape(b, n_chunks, BAND, h, dh)
    scores = jnp.einsum("bcqhd,bckhd->bhcqk", qc, kb).astype(jnp.float32) * (dh ** -0.5)
    scores = scores + rel_bias_band(rel_bias).astype(jnp.float32)[None, :, None]
    key_chunk = (jnp.arange(n_chunks)[:, None] - N_PREV_CHUNKS
                 + (jnp.arange(BAND) // CHUNK)[None, :])
    scores = jnp.where((key_chunk >= 0)[None, None, :, None, :], scores, -jnp.inf)
    probs = jax.nn.softmax(scores, axis=-1).astype(v.dtype)
    out = jnp.einsum("bhcqk,bckhd->bcqhd", probs, vb)
    return out.reshape(b, s, h * dh)


def stick_breaking_attention(q, k, v):
    b, s, h, dh = q.shape
    scale = dh ** -0.5
    outs = []
    for blk in range(s // SB_BLOCK):
        q_start = blk * SB_BLOCK
        k_end = q_start + SB_BLOCK
        logits = jnp.einsum("bqhd,bkhd->bhqk", q[:, q_start:k_end],
                            k[:, :k_end]).astype(jnp.float32) * scale
        strict = (q_start + jnp.arange(SB_BLOCK))[:, None] > jnp.arange(k_end)[None, :]
        log_beta = jax.nn.log_sigmoid(logits)
        log_keep = jnp.where(strict, jax.nn.log_sigmoid(-logits), 0.0)
        log_tail = lax.cumsum(log_keep, axis=3, reverse=True) - log_keep
        weights = jnp.where(strict, jnp.exp(log_beta + log_tail), 0.0).astype(v.dtype)
        outs.append(jnp.einsum("bhqk,bkhd->bqhd", weights, v[:, :k_end]))
    return jnp.concatenate(outs, axis=1).reshape(b, s, h * dh)


def causal_depthwise_conv(u, w, bias):
    s = u.shape[1]
    up = jnp.pad(u, ((0, 0), (CONV_WIDTH - 1, 0), (0, 0)))
    y = w[0] * up[:, 0:s]
    for i in range(1, CONV_WIDTH):
        y = y + w[i] * up[:, i:i + s]
    return y + bias


def hybrid_layer(x, c_act, w_ada, b_ada, w_in, rel_bias, g_a, g_b, w_out,
                 w_up, conv_w, conv_b, w_down):
    b, s, _ = x.shape
    mod = c_act @ w_ada + b_ada
    shift_mix, scale_mix, gate_mix, shift_ffn, scale_ffn, gate_ffn = jnp.split(mod, N_MOD, axis=-1)

    h = modulate(rms_norm(x), shift_mix, scale_mix)
    proj = h @ w_in
    cuts = [W_A, 2 * W_A, 3 * W_A, 3 * W_A + W_B, 3 * W_A + 2 * W_B]
    qa, ka, va, qb, kb, vb = jnp.split(proj, cuts, axis=-1)
    oa = chunked_rel_attention(qa.reshape(b, s, H_A, HEAD_DIM), ka.reshape(b, s, H_A, HEAD_DIM),
                               va.reshape(b, s, H_A, HEAD_DIM), rel_bias)
    ob = stick_breaking_attention(qb.reshape(b, s, H_B, HEAD_DIM), kb.reshape(b, s, H_B, HEAD_DIM),
                                  vb.reshape(b, s, H_B, HEAD_DIM))
    mixed = jnp.concatenate([rms_norm(oa) * g_a, rms_norm(ob) * g_b], axis=-1) @ w_out
    x = x + gate_mix[:, None, :] * mixed

    h = modulate(rms_norm(x), shift_ffn, scale_ffn)
    gate, val = jnp.split(causal_depthwise_conv(h @ w_up, conv_w, conv_b), 2, axis=-1)
    x = x + gate_ffn[:, None, :] * ((jax.nn.silu(gate) * val) @ w_down)
    return x


def setup_inputs(seed: int = 0) -> dict:
    key = jax.random.key(seed)
    ks = jax.random.split(key, 14)
    f32 = jnp.float32
    nrm = lambda k, shape: jax.random.normal(k, shape, dtype=f32)
    return {
        "x": nrm(ks[0], (BATCH, SEQ, D_MODEL)),
        "c": nrm(ks[1], (BATCH, D_MODEL)),
        "w_ada": nrm(ks[2], (DEPTH, D_MODEL, N_MOD * D_MODEL)) * D_MODEL ** -0.5,
        "b_ada": nrm(ks[3], (DEPTH, N_MOD * D_MODEL)) * 0.02,
        "w_in": nrm(ks[4], (DEPTH, D_MODEL, W_IN)) * D_MODEL ** -0.5,
        "rel_bias": nrm(ks[5], (DEPTH, H_A, 2 * REL_CLIP + 1)) * 0.5,
        "g_a": 1.0 + 0.1 * nrm(ks[6], (DEPTH, W_A)),
        "g_b": 1.0 + 0.1 * nrm(ks[7], (DEPTH, W_B)),
        "w_out": nrm(ks[8], (DEPTH, W_MIX, D_MODEL)) * W_MIX ** -0.5,
        "w_up": nrm(ks[9], (DEPTH, D_MODEL, 2 * D_FF)) * D_MODEL ** -0.5,
        "conv_w": nrm(ks[10], (DEPTH, CONV_WIDTH, 2 * D_FF)) * CONV_WIDTH ** -0.5,
        "conv_b": nrm(ks[11], (DEPTH, 2 * D_FF)) * 0.02,
        "w_down": nrm(ks[12], (DEPTH, D_FF, D_MODEL)) * D_FF ** -0.5,
        "final_g": 1.0 + 0.1 * nrm(ks[13], (D_MODEL,)),
    }


def reference(x, c, w_ada, b_ada, w_in, rel_bias, g_a, g_b, w_out, w_up, conv_w,
              conv_b, w_down, final_g):
    c_act = jax.nn.silu(c)
    for l in range(DEPTH):
        x = hybrid_layer(x, c_act, w_ada[l], b_ada[l], w_in[l], rel_bias[l], g_a[l], g_b[l],
                         w_out[l], w_up[l], conv_w[l], conv_b[l], w_down[l])
    return rms_norm(x) * final_g
```

```python
import numpy as np
import concourse.bass as bass
import concourse.mybir as mybir
from concourse.bass_utils import run_bass_kernel_spmd

F32 = mybir.dt.float32
BF16 = mybir.dt.bfloat16
AF = mybir.ActivationFunctionType
ALU = mybir.AluOpType

D = 1024
KC = 8
NH = 16
DH = 64
DFF = 2816
NJ = 22
EPS = 1e-6
NEG = -30000.0


import os as _os
STRICT = _os.environ.get("KSTRICT", "1") != "0"


class Buf:
    __slots__ = ("name", "writers", "readers", "excl")

    def __init__(self, name, excl=False):
        self.name = name
        self.writers = {}
        self.readers = {}
        self.excl = excl


class SemObj:
    __slots__ = ("h", "count", "name", "is_dma")

    def __init__(self, h, name):
        self.h = h
        self.count = 0
        self.name = name
        self.is_dma = False


class Eng:
    def __init__(self, name, handle, sem):
        self.name = name
        self.handle = handle
        self.sem = sem
        self.ops = []
        self.seen = {}


class Prog:
    def __init__(self, nc):
        self.nc = nc
        self.engs = {}
        self.sems = []

    def new_sem(self, name):
        s = SemObj(self.nc.alloc_semaphore(name=name), name)
        self.sems.append(s)
        return s

    def add_engine(self, name, handle):
        e = Eng(name, handle, self.new_sem("e_" + name))
        self.engs[name] = e
        return e

    def op(self, eng, fn, reads=(), writes=(), dma_sem=None, extra=()):
        if isinstance(fn, tuple):
            fn = [fn]
        ex_r = [b for b in reads if b.excl]
        if ex_r:
            reads = [b for b in reads if not b.excl]
            writes = list(writes) + [b for b in ex_r if b not in writes]
        e = self.engs[eng]
        is_dma = dma_sem is not None
        if not is_dma and e.sem.count >= 60000:
            e.sem = self.new_sem("e_%s_%d" % (eng, len(self.sems)))
        waits = {}

        def addw(s, v):
            if waits.get(s, -1) < v:
                waits[s] = v

        for b in reads:
            for s, v in b.writers.items():
                addw(s, v)
        for b in writes:
            skip_same = (not STRICT) or (eng == "pe" and b.excl)
            for s, v in b.writers.items():
                if s is e.sem and not is_dma and skip_same:
                    continue
                addw(s, v)
            for s, v in b.readers.items():
                if s is e.sem and not is_dma and skip_same:
                    continue
                addw(s, v)
        for s, v in extra:
            addw(s, v)
        wl = []
        for s, v in list(waits.items()):
            if s.is_dma:
                v = s.count
            if e.seen.get(s, -1) >= v:
                continue
            e.seen[s] = v
            wl.append((s, v))
        if is_dma:
            dma_sem.is_dma = True
            dma_sem.count += 16
            tok = (dma_sem, dma_sem.count)
            e.ops.append((wl, fn, dma_sem, 16))
        else:
            e.sem.count += 1
            tok = (e.sem, e.sem.count)
            e.ops.append((wl, fn, e.sem, 1))
        for b in reads:
            if b.readers.get(tok[0], -1) < tok[1]:
                b.readers[tok[0]] = tok[1]
        for b in writes:
            if b.writers.get(tok[0], -1) < tok[1]:
                b.writers[tok[0]] = tok[1]
        return tok

    def barrier(self):
        for e in self.engs.values():
            wl = []
            for s in self.sems:
                if s.count > 0 and e.seen.get(s, -1) < s.count:
                    e.seen[s] = s.count
                    wl.append((s, s.count))
            if wl:
                e.ops.append((wl, None, None, 0))

    def emit(self):
        nc = self.nc
        self.barrier()
        with nc.Block() as block:
            def mk(e):
                def body(h):
                    for wl, fn, isem, ival in e.ops:
                        for s, v in wl:
                            h.wait_ge(s.h, v)
                        if fn is not None:
                            ins = None
                            for (m, a, k) in fn:
                                ins = getattr(h, m)(*a, **k)
                            ins.then_inc(isem.h, ival)
                return body
            decs = {"pe": block.tensor, "act": block.scalar, "dve": block.vector,
                    "pool": block.gpsimd, "sp": block.sync}
            for name, e in self.engs.items():
                decs[name](mk(e))


def I(m, *a, **k):
    return (m, a, k)


class Arena:
    def __init__(self, nc, base, limit):
        self.nc = nc
        self.base = base
        self.top = base
        self.limit = limit
        self.n = 0

    def t(self, name, shape, dtype):
        esz = 4 if dtype == F32 else 2
        n = esz
        for s in shape[1:]:
            n *= s
        off = (self.top + 63) // 64 * 64
        assert off + n <= self.limit, ("SBUF overflow", name, off, n, self.limit)
        self.top = off + n
        self.n += 1
        return self.nc.alloc_sbuf_tensor_at("%s_%d" % (name, self.n), list(shape), dtype, offset=off)

    def reset(self):
        self.top = self.base


def build(S, DEPTH, dbg=False, stop=None):
    import os
    nc = bass.Bass("TRN2", target_bir_lowering=False)
    NT = S // 512
    NB = S // 128
    skind = "ExternalOutput" if dbg else "Internal"

    def din(name, shape, dt=F32):
        return nc.dram_tensor(name, list(shape), dt, kind="ExternalInput")

    x_d = din("x", [S, D])
    c_d = din("c", [8, 128])
    wada_d = din("w_ada", [DEPTH, D, 6 * D])
    bada_d = din("b_ada", [DEPTH, 6 * D])
    win_d = din("w_in", [DEPTH, D, 3 * D])
    relb_d = din("rel_bias", [DEPTH, 8, 257])
    g_d = din("g", [DEPTH, 16, 64])
    wout_d = din("w_out", [DEPTH, D, D])
    wup_d = din("w_up", [DEPTH, D, 2 * DFF])
    convw_d = din("conv_w", [DEPTH, 132, 128])
    convb_d = din("conv_b", [DEPTH, 44, 128])
    wdn_d = din("w_down", [DEPTH, DFF, D])
    fg_d = din("final_g", [1, D])
    out_d = nc.dram_tensor("out", [S, D], F32, kind="ExternalOutput")

    def dscr(name, shape, dt):
        return nc.dram_tensor(name, list(shape), dt, kind=skind)

    winb_d = nc.dram_tensor("win_b", [DEPTH, D, 3 * D], BF16, kind="Internal")
    wupb_d = nc.dram_tensor("wup_b", [DEPTH, NJ, 128, KC, 2, 128], BF16, kind="Internal")
    wdnb_d = nc.dram_tensor("wdn_b", [DEPTH, DFF, D], BF16, kind="Internal")
    ext_d = nc.dram_tensor("ext_s", [8, 640], F32, kind="Internal")
    qT_d = dscr("qT_s", [NH * DH, S], BF16)
    kT_d = dscr("kT_s", [NH * DH, S], BF16)
    v_d = dscr("v_s", [NH, 128, NB, DH], BF16)
    aT_d = dscr("aT_s", [NH, DH, S], BF16)
    xs_d = dscr("xs_s", [S, D], F32)

    P = Prog(nc)
    for n, h in (("pe", nc.tensor), ("act", nc.scalar), ("dve", nc.vector),
                 ("pool", nc.gpsimd), ("sp", nc.sync)):
        P.add_engine(n, h)
    op = P.op

    SB0 = 16640
    PERS = SB0 + 67584
    pa = Arena(nc, SB0, PERS)
    ident_bf = pa.t("ident_bf", [128, 128], BF16)
    ident_f = pa.t("ident_f", [128, 128], F32)
    Jf = pa.t("Jf", [128, 128], F32)
    triN = pa.t("triN", [128, 128], BF16)
    onesN = pa.t("onesN", [128, 128], BF16)
    dmask = pa.t("dmask", [128, 4, 512], BF16)
    ones_col = pa.t("ones_col", [128, 1], F32)
    ones_row = pa.t("ones_row", [1, 128], F32)
    cactT = pa.t("cactT", [128, 8], F32)
    cact_bc = pa.t("cact_bc", [128, 8, 128], F32)
    shsc = pa.t("shsc", [128, 4, 8], F32)
    gate_b = pa.t("gate_b", [128, 2, 1024], F32)
    bT = pa.t("bT", [128, 48], F32)
    gT = pa.t("gT", [128, 8], F32)
    cwT = pa.t("cwT", [128, 132], F32)
    cbT = pa.t("cbT", [128, 44], F32)
    biasT = pa.t("biasT", [128, 8, 256], F32)
    cconst = pa.t("cconst", [128, 8], F32)
    carry = pa.t("carry", [128, 2, 44, 2], F32)
    fg_b = pa.t("fg_b", [128, 1024], F32)
    wout2 = pa.t("wout2", [128, 8, 1024], BF16)
    B_const = Buf("const")
    B_layer = Buf("layer")
    B_wout2 = Buf("wout2")
    B_carry = [Buf("carry0"), Buf("carry1")]

    ar = Arena(nc, PERS, 229000)

    pg = nc.alloc_psum_tensor("pg", [128, 6 * 512], F32)
    ptb = [nc.alloc_psum_tensor("ptb0", [128, 1024], BF16), nc.alloc_psum_tensor("ptb1", [128, 1024], BF16)]
    BK = [Buf("bank%d" % i, excl=True) for i in range(6)]
    B_pt = [Buf("pt0", excl=True), Buf("pt1", excl=True)]

    def bank(i, n=512):
        return pg[:, i * 512:i * 512 + n]

    NFILL = int(os.environ.get("NFILL", "0"))
    fill_out = ptb[1].bitcast(F32)

    def filler(n=None):
        n = NFILL if n is None else n
        if n <= 0:
            return
        op("pe", [I("matmul", fill_out[:, 0:512], ident_bf[:], dmask[:, 0, :], start=True, stop=True) for _ in range(n)],
           reads=[B_const], writes=[B_pt[1]])

    B_x = Buf("x_in")
    B_winb = [Buf("winb%d" % l) for l in range(DEPTH)]
    B_wupb = [Buf("wupb%d" % l) for l in range(DEPTH)]
    B_wdnb = [Buf("wdnb%d" % l) for l in range(DEPTH)]
    B_ext = Buf("ext")
    B_qT = Buf("qT"); B_kT = Buf("kT"); B_v = Buf("v"); B_aT = Buf("aT"); B_xs = Buf("xs"); B_out = Buf("out")

    cast_sems = [P.new_sem("cast%d" % i) for i in range(4)]
    cast_n = [0]

    def cast_op(dst, src, B, sem_unused=None):
        sem = cast_sems[cast_n[0] % 4]
        cast_n[0] += 1
        ex = [(sem, sem.count)] if sem.count > 0 else []
        tok = op("pool", I("dma_start", out=dst, in_=src), dma_sem=sem, extra=ex)
        B.writers[tok[0]] = tok[1]

    def cast_win(l):
        for kc in range(KC):
            src = win_d[l, kc * 128:(kc + 1) * 128, :].rearrange("p (a b) -> p a b", b=1024)
            dst = winb_d[l, kc * 128:(kc + 1) * 128, :].rearrange("p (a b) -> p a b", b=1024)
            cast_op(dst, src, B_winb[l])

    def cast_ffn(l):
        for j in range(NJ if not os.environ.get("NOWUP") else 0):
            for g in range(2):
                src = wup_d[l, :, g * DFF + j * 128: g * DFF + (j + 1) * 128].rearrange("(kc p) c -> p kc c", p=128)
                dst = wupb_d[l, j, :, :, g, :]
                cast_op(dst, src, B_wupb[l])
        for j in range(NJ if not os.environ.get("NOWDN") else 0):
            src = wdn_d[l, j * 128:(j + 1) * 128, :]
            dst = wdnb_d[l, j * 128:(j + 1) * 128, :]
            cast_op(dst, src, B_wdnb[l])

    op("pool", [I("memset", ident_bf[:], 0.0), I("memset", ident_f[:], 0.0), I("memset", Jf[:], 0.0),
                I("memset", triN[:], -1.0), I("memset", onesN[:], -1.0), I("memset", ones_col[:], 1.0),
                I("memset", ones_row[:], 1.0), I("memset", dmask[:], 1.0)], writes=[B_const])
    ins = [
        I("affine_select", out=ident_bf[:], in_=ident_bf[:], pattern=[[-1, 128]], compare_op=ALU.not_equal,
          fill=1.0, base=0, channel_multiplier=1),
        I("affine_select", out=ident_f[:], in_=ident_f[:], pattern=[[-1, 128]], compare_op=ALU.not_equal,
          fill=1.0, base=0, channel_multiplier=1),
        I("affine_select", out=Jf[:], in_=Jf[:], pattern=[[1, 128]], compare_op=ALU.not_equal,
          fill=1.0, base=-127, channel_multiplier=1),
        I("affine_select", out=triN[:], in_=triN[:], pattern=[[-1, 128]], compare_op=ALU.is_ge,
          fill=0.0, base=0, channel_multiplier=1),
    ]
    for i in range(4):
        ins.append(I("affine_select", out=dmask[:, i, :], in_=dmask[:, i, :], pattern=[[1, 512]],
                     compare_op=ALU.is_gt, fill=0.0, base=-128 * i, channel_multiplier=-1))
    op("pool", ins, reads=[B_const], writes=[B_const])
    cast_win(0)

    ar.reset()
    c_sb = ar.t("c_sb", [8, 128], F32)
    c_act = ar.t("c_act", [8, 128], F32)
    fg_row = ar.t("fg_row", [1, 1024], F32)
    B_c = Buf("c_sb"); B_ca = Buf("c_act"); B_fgr = Buf("fg_row")
    s_misc = P.new_sem("misc")
    op("sp", I("dma_start", out=c_sb[:], in_=c_d[:, :]), writes=[B_c], dma_sem=s_misc)
    op("sp", I("dma_start", out=fg_row[:], in_=fg_d[:, :]), writes=[B_fgr], dma_sem=s_misc)
    op("act", I("activation", c_act[:], c_sb[:], AF.Silu), reads=[B_c], writes=[B_ca])
    op("pe", I("transpose", out=bank(0, 8), in_=c_act[:], identity=ident_f[0:8, 0:8]),
       reads=[B_ca, B_const], writes=[BK[0]])
    op("dve", I("tensor_copy", cactT[:], bank(0, 8)), reads=[BK[0]], writes=[B_const])
    op("dve", [I("tensor_copy", cact_bc[:, kc, :], cactT[:, kc:kc + 1].to_broadcast([128, 128])) for kc in range(KC)],
       reads=[B_const], writes=[B_const])
    op("pe", [I("matmul", bank(1), ones_row[0:1, :], fg_row[0:1, 0:512], start=True, stop=True),
              I("matmul", bank(2), ones_row[0:1, :], fg_row[0:1, 512:1024], start=True, stop=True)],
       reads=[B_fgr, B_const], writes=[BK[1], BK[2]])
    op("dve", [I("tensor_copy", fg_b[:, 0:512], bank(1)), I("tensor_copy", fg_b[:, 512:1024], bank(2))],
       reads=[BK[1], BK[2]], writes=[B_const])

    if stop == "p0":
        P.emit()
        return nc
    s_wa = [P.new_sem("wa%d" % i) for i in range(4)]
    s_wo = [P.new_sem("wo0"), P.new_sem("wo1")]
    s_small = P.new_sem("small")
    s_win = P.new_sem("win")
    s_xt = [P.new_sem("xt0"), P.new_sem("xt1")]
    s_qk = [P.new_sem("qk%d" % i) for i in range(4)]
    s_vst = [P.new_sem("vst%d" % i) for i in range(4)]
    s_hd = [P.new_sem("hd0"), P.new_sem("hd1")]
    s_uT = [P.new_sem("uT0"), P.new_sem("uT1")]
    s_x3l = [P.new_sem("x3a"), P.new_sem("x3b")]
    s_x3o = P.new_sem("x3o")
    s_aTt = P.new_sem("aTt")
    s_wup = [P.new_sem("wup%d" % i) for i in range(3)]
    s_wdn = [P.new_sem("wdn%d" % i) for i in range(3)]

    def norm_to_hT(xs_, B_xtile, s, vi_shift, vi_scale, sc, W, part="ab", hT=None, B_hT=None, ci=0):
        xh = W["xhat"][s % 2]
        B_xh = W["B_xhat"][s % 2]
        hT = W["hT"] if hT is None else hT
        B_hT = W["B_hT"] if B_hT is None else B_hT
        B_sc = W["B_sc"][ci]
        if "a" in part:
            op("act", I("activation", W["junk"][:], xs_, AF.Square, accum_out=sc),
               reads=[B_xtile], writes=[W["B_junk"], B_sc])
            op("act", I("activation", sc, sc, AF.Ln, bias=W["eps"][:, 0:1], scale=1.0 / D), reads=[B_sc, W["B_stat"]], writes=[B_sc])
            op("act", I("activation", sc, sc, AF.Exp, scale=-0.5), reads=[B_sc], writes=[B_sc])
            op("dve", I("tensor_scalar", out=xh[:], in0=xs_, scalar1=sc, scalar2=None, op0=ALU.mult),
               reads=[B_xtile, B_sc], writes=[B_xh])
        if "b" in part:
            for half in range(2):
                op("pe", [I("transpose", out=ptb[half][:, k4 * 128:(k4 + 1) * 128],
                            in_=xh[:, (half * 4 + k4) * 128:(half * 4 + k4 + 1) * 128], identity=ident_bf[:])
                          for k4 in range(4)],
                   reads=[B_xh, B_const], writes=[B_pt[half]])
                for k4 in range(4):
                    kc = half * 4 + k4
                    op("dve", I("tensor_scalar", out=hT[:, kc, s * 128:(s + 1) * 128], in0=ptb[half][:, k4 * 128:(k4 + 1) * 128],
                                scalar1=shsc[:, vi_scale, kc:kc + 1], scalar2=shsc[:, vi_shift, kc:kc + 1],
                                op0=ALU.mult, op1=ALU.add),
                       reads=[B_pt[half], B_layer], writes=[B_hT])

    for l in range(DEPTH):
        x_src = x_d if l == 0 else xs_d
        B_xsrc = B_x if l == 0 else B_xs
        last = (l == DEPTH - 1)
        x_dst = out_d if last else xs_d
        B_xdst = B_out if last else B_xs

        P.barrier()
        ar.reset()
        wa = [ar.t("wa", [128, KC, 512], F32) for _ in range(4)]
        B_wa = [Buf("wa%d" % i) for i in range(4)]
        brow = ar.t("brow", [1, 2, 1024], F32)
        b48 = ar.t("b48", [48, 128], F32)
        g16 = ar.t("g16", [8, 128], F32)
        cw1 = ar.t("cw1", [128, 128], F32)
        cw2 = ar.t("cw2", [4, 128], F32)
        cb44 = ar.t("cb44", [44, 128], F32)
        ext_sb = ar.t("ext_sb", [1, 8, 640], F32)
        hank = ar.t("hank", [128, 8, 2, 128], F32)
        wo_st = [ar.t("wo_st", [128, 1024], F32) for _ in range(2)]
        B_wo = [Buf("wo0"), Buf("wo1")]
        B_small = Buf("small")
        B_exts = Buf("ext_sb"); B_hank = Buf("hank")

        op("sp", I("dma_start", out=b48[:], in_=bada_d[l].rearrange("(a b) -> a b", b=128)),
           writes=[B_small], dma_sem=s_small)
        op("sp", I("dma_start", out=brow[0:1, 0, :], in_=bada_d[l:l + 1, 2 * D:3 * D]), writes=[B_small], dma_sem=s_small)
        op("sp", I("dma_start", out=brow[0:1, 1, :], in_=bada_d[l:l + 1, 5 * D:6 * D]), writes=[B_small], dma_sem=s_small)
        op("sp", I("dma_start", out=g16[:], in_=g_d[l].rearrange("(a two) d -> a (two d)", two=2)), writes=[B_small], dma_sem=s_small)
        op("sp", I("dma_start", out=cw1[:], in_=convw_d[l, 0:128, :]), writes=[B_small], dma_sem=s_small)
        op("sp", I("dma_start", out=cw2[:], in_=convw_d[l, 128:132, :]), writes=[B_small], dma_sem=s_small)
        op("sp", I("dma_start", out=cb44[:], in_=convb_d[l]), writes=[B_small], dma_sem=s_small)
        op("sp", I("dma_start", out=ext_sb[0:1, :, 0:256], in_=relb_d[l:l + 1, :, 1:257]), writes=[B_exts], dma_sem=s_small)

        op("pe", [I("transpose", out=bank(0, 48), in_=b48[:], identity=ident_f[0:48, 0:48]),
                  I("transpose", out=pg[:, 512:512 + 8], in_=g16[:], identity=ident_f[0:8, 0:8]),
                  I("transpose", out=bank(2, 128), in_=cw1[:], identity=ident_f[:]),
                  I("transpose", out=pg[:, 2 * 512 + 128:2 * 512 + 132], in_=cw2[:], identity=ident_f[0:4, 0:4]),
                  I("transpose", out=bank(3, 44), in_=cb44[:], identity=ident_f[0:44, 0:44])],
           reads=[B_small, B_const], writes=[BK[0], BK[1], BK[2], BK[3]])
        op("dve", [I("tensor_copy", bT[:], bank(0, 48)),
                   I("tensor_copy", gT[:], pg[:, 512:512 + 8]),
                   I("tensor_copy", cwT[:], bank(2, 132)),
                   I("tensor_copy", cbT[:], bank(3, 44))],
           reads=[BK[0], BK[1], BK[2], BK[3]], writes=[B_layer])
        op("pool", I("memset", carry[:], 0.0), writes=[B_carry[0], B_carry[1]])

        op("dve", I("tensor_copy", ext_sb[0:1, :, 256:640], ext_sb[0:1, :, 255:256].to_broadcast([1, 8, 384])),
           reads=[B_exts], writes=[B_exts])
        op("sp", I("dma_start", out=ext_d[:, :], in_=ext_sb[0:1, :, :]), reads=[B_exts], writes=[B_ext], dma_sem=s_small)
        for hh in range(8):
            for bi, base in ((0, 128), (1, 0)):
                src = bass.AP(ext_d, hh * 640 + base, [[1, 128], [1, 128]])
                op("sp", I("dma_start", out=hank[:, hh, bi, :], in_=src), reads=[B_ext], writes=[B_hank], dma_sem=s_small)
        srcc = bass.AP(ext_d, 300, [[1, 128], [640, 8], [1, 1]])
        op("sp", I("dma_start", out=cconst[:].rearrange("p (a b) -> p a b", b=1), in_=srcc, allow_slow_non_contiguous=True),
           reads=[B_ext], writes=[B_layer], dma_sem=s_small)
        for hh in range(8):
            bk = 4 + (hh % 2)
            op("pe", I("matmul", bank(bk, 256), Jf[:], hank[:, hh, :, :].rearrange("p a b -> p (a b)"), start=True, stop=True),
               reads=[B_hank, B_const], writes=[BK[bk]])
            op("dve", I("tensor_copy", biasT[:, hh, :], bank(bk, 256)), reads=[BK[bk]], writes=[B_layer])
        op("pool", I("memset", biasT[64:128, :, 128:192], NEG), writes=[B_layer])

        for cb in range(12):
            which, half = cb // 2, cb % 2
            sl = cb % 4
            src = wada_d[l, :, cb * 512:(cb + 1) * 512].rearrange("(kc p) c -> p kc c", p=128)
            op("sp", I("dma_start", out=wa[sl][:], in_=src), writes=[B_wa[sl]], dma_sem=s_wa[sl])
            if which in (2, 5):
                gi = 0 if which == 2 else 1
                ins = [I("matmul", bank(5), cact_bc[:, kc, :], wa[sl][:, kc, :], start=(kc == 0), stop=False)
                       for kc in range(KC)]
                ins.append(I("matmul", bank(5), ones_row[0:1, :], brow[0:1, gi, half * 512:(half + 1) * 512],
                             start=False, stop=True))
                op("pe", ins, reads=[B_wa[sl], B_const, B_small], writes=[BK[5]])
                op("dve", I("tensor_copy", gate_b[:, gi, half * 512:(half + 1) * 512], bank(5)),
                   reads=[BK[5]], writes=[B_layer])
            else:
                vi = {0: 0, 1: 1, 3: 2, 4: 3}[which]
                ins = []
                for ch in range(4):
                    for kc in range(KC):
                        ins.append(I("matmul", pg[:, 3 * 512 + ch:3 * 512 + ch + 1], wa[sl][:, kc, ch * 128:(ch + 1) * 128],
                                     cactT[:, kc:kc + 1], start=(kc == 0), stop=(kc == KC - 1)))
                op("pe", ins, reads=[B_wa[sl], B_const], writes=[BK[3]])
                addc = 1.0 if vi in (1, 3) else 0.0
                op("dve", I("scalar_tensor_tensor", out=shsc[:, vi, half * 4:(half + 1) * 4], in0=pg[:, 3 * 512:3 * 512 + 4],
                            scalar=addc, in1=bT[:, cb * 4:(cb + 1) * 4], op0=ALU.add, op1=ALU.add),
                   reads=[BK[3], B_layer], writes=[B_layer])

        for hp in range(8):
            sl = hp % 2
            op("sp", I("dma_start", out=wo_st[sl][:], in_=wout_d[l, hp * 128:(hp + 1) * 128, :]),
               writes=[B_wo[sl]], dma_sem=s_wo[sl])
            op("dve", I("scalar_tensor_tensor", out=wout2[:, hp, :], in0=wo_st[sl][:], scalar=gT[:, hp:hp + 1],
                        in1=gate_b[:, 0, :], op0=ALU.mult, op1=ALU.mult),
               reads=[B_wo[sl], B_layer], writes=[B_wout2])

        if stop == "L0":
            P.emit()
            return nc
        P.barrier()
        ar.reset()
        win_sb = ar.t("win_sb", [128, KC, 3 * D], BF16)
        B_win = Buf("win_sb")
        xt = [ar.t("xt", [128, 4, D], F32) for _ in range(2)]
        B_xt = [Buf("xt0"), Buf("xt1")]
        W1 = dict(junk=ar.t("junk", [128, D], BF16), B_junk=Buf("junk"),
                  xhat=[ar.t("xhat", [128, D], BF16) for _ in range(2)], B_xhat=[Buf("xh0"), Buf("xh1")],
                  hT=ar.t("hT", [128, KC, 512], BF16), B_hT=Buf("hT"),
                  B_stat=Buf("stat"), eps=ar.t("eps1", [128, 1], F32), B_sc=[Buf("sc%d" % i) for i in range(16)])
        stat = ar.t("stat", [128, 16], F32)
        hT = W1["hT"]; B_hT = W1["B_hT"]
        op("pool", I("memset", W1["eps"][:], EPS), writes=[W1["B_stat"]])
        qk_st = [ar.t("qk_st", [128, 512], BF16) for _ in range(4)]
        B_qk = [Buf("qk%d" % i) for i in range(4)]
        v_st = [ar.t("v_st", [128, 8, DH], BF16) for _ in range(4)]
        B_vst = [Buf("vst%d" % i) for i in range(4)]

        for kc in range(KC if not os.environ.get("NOWIN") else 0):
            op("sp", I("dma_start", out=win_sb[:, kc, :], in_=winb_d[l, kc * 128:(kc + 1) * 128, :]),
               reads=[B_winb[l]], writes=[B_win], dma_sem=s_win)

        if l == 0 and not os.environ.get("NOCAST"):
            cast_ffn(0)
            for l2 in range(1, DEPTH):
                cast_win(l2)
                cast_ffn(l2)

        def load_x(T):
            sl = T % 2
            src = x_src[T * 512:(T + 1) * 512, :].rearrange("(s p) d -> p s d", p=128)
            op("sp", I("dma_start", out=xt[sl][:], in_=src), reads=[B_xsrc], writes=[B_xt[sl]], dma_sem=s_xt[sl])

        hT2 = ar.t("hT_b", [128, KC, 512], BF16)
        hTs = [hT, hT2]
        B_hTs = [B_hT, Buf("hT_b")]

        def norm_a(T, s):
            ci = (T % 2) * 4 + s
            norm_to_hT(xt[T % 2][:, s, :], B_xt[T % 2], s, 0, 1, stat[:, ci:ci + 1], W1, part="a", ci=ci)

        def norm_b(T, s):
            ci = (T % 2) * 4 + s
            norm_to_hT(xt[T % 2][:, s, :], B_xt[T % 2], s, 0, 1, stat[:, ci:ci + 1], W1, part="b",
                       hT=hTs[T % 2], B_hT=B_hTs[T % 2], ci=ci)

        load_x(0)
        if NT > 1:
            load_x(1)
        for s in range(4):
            norm_a(0, s)
            norm_b(0, s)
        stn = 0
        for T in range(NT):
            hT = hTs[T % 2]
            B_hT = B_hTs[T % 2]
            if T + 2 < NT:
                load_x(T + 2)
            gi = [0]

            def after_group():
                g_ = gi[0]
                gi[0] += 1
                if T + 1 < NT:
                    if g_ % 6 == 0:
                        norm_a(T + 1, g_ // 6)
                    if g_ % 6 == 4:
                        norm_b(T + 1, g_ // 6)
            P1PART = int(os.environ.get("P1PART", "3"))
            for cbk in range(24 if P1PART >= 2 else 0):
                typ = cbk // 4
                if typ in (2, 5):
                    continue
                bk = cbk % 2
                grp = 0 if typ < 3 else 1
                isq = typ in (0, 3)
                hp = (cbk % 4) + 4 * grp
                op("pe", [I("matmul", bank(bk), win_sb[:, kc, cbk * 128:(cbk + 1) * 128], hT[:, kc, :],
                            start=(kc == 0), stop=(kc == KC - 1)) for kc in range(KC)],
                   reads=[B_win, B_hT], writes=[BK[bk]])
                sl = stn % 4
                stn += 1
                scl = 0.125 if isq else 1.0
                if stn % 2:
                    op("act", I("activation", qk_st[sl][:], bank(bk), AF.Copy, scale=scl), reads=[BK[bk]], writes=[B_qk[sl]])
                else:
                    op("dve", I("tensor_scalar", out=qk_st[sl][:], in0=bank(bk), scalar1=scl, scalar2=None, op0=ALU.mult),
                       reads=[BK[bk]], writes=[B_qk[sl]])
                dstt = qT_d if isq else kT_d
                B_dst = B_qT if isq else B_kT
                op("sp", I("dma_start", out=dstt[hp * 128:(hp + 1) * 128, T * 512:(T + 1) * 512], in_=qk_st[sl][:]),
                   reads=[B_qk[sl]], writes=[B_dst], dma_sem=s_qk[sl])
                after_group()
            for grp in range(2 if P1PART >= 3 else 0):
                c0 = (2 if grp == 0 else 5) * 512
                for s in range(4):
                    bk = 2 + (s % 2)
                    op("pe", [I("matmul", bank(bk), hT[:, kc, s * 128:(s + 1) * 128], win_sb[:, kc, c0:c0 + 512],
                                start=(kc == 0), stop=(kc == KC - 1)) for kc in range(KC)],
                       reads=[B_win, B_hT], writes=[BK[bk]])
                    sl = (grp * 4 + s) % 4
                    op("act", I("activation", v_st[sl][:].rearrange("p a b -> p (a b)"), bank(bk), AF.Copy),
                       reads=[BK[bk]], writes=[B_vst[sl]])
                    op("sp", I("dma_start", out=v_d[grp * 8:(grp + 1) * 8, :, 4 * T + s, :].rearrange("h p d -> p h d"),
                               in_=v_st[sl][:]),
                       reads=[B_vst[sl]], writes=[B_v], dma_sem=s_vst[sl])
                    after_group()

        if stop == "p1":
            P.emit()
            return nc
        P.barrier()
        ar.reset()
        qTh = [ar.t("qTh", [64, S], BF16) for _ in range(2)]
        kTh = [ar.t("kTh", [64, S], BF16) for _ in range(2)]
        vh = [ar.t("vh", [128, NB, 65], BF16) for _ in range(2)]
        B_qh = [Buf("qh0"), Buf("qh1")]; B_kh = [Buf("kh0"), Buf("kh1")]; B_vh = [Buf("vh0"), Buf("vh1")]
        uT = [ar.t("uT", [64, S], BF16) for _ in range(2)]
        B_uT = [Buf("uT0"), Buf("uT1")]
        tmpb = [ar.t("tmpb", [128, 256], F32) for _ in range(2)]
        B_tmpb = [Buf("tmpb0"), Buf("tmpb1")]
        PT = [ar.t("PT", [128, 5, 128], BF16) for _ in range(3)]
        B_PT = [Buf("PT%d" % i) for i in range(3)]
        rcp = ar.t("rcp", [128, 4], F32)
        B_rcp = Buf("rcp")
        ubf = [ar.t("ubf", [128, 64], BF16) for _ in range(2)]
        B_ubf = [Buf("ubf0"), Buf("ubf1")]
        e_sb = [ar.t("e_sb", [128, 1024], F32) for _ in range(3)]
        B_e = [Buf("e%d" % i) for i in range(3)]
        sp_sb = [ar.t("sp_sb", [128, 1024], BF16) for _ in range(4)]
        B_sp = [Buf("sp%d" % i) for i in range(4)]
        acc = ar.t("acc", [128, 512], BF16)
        B_acc = Buf("acc")
        W_sb = [ar.t("W_sb", [128, 1024], BF16) for _ in range(4)]
        B_W = [Buf("W%d" % i) for i in range(4)]
        one1 = ar.t("one1", [128, 1], F32)
        B_one1 = Buf("one1")
        op("pool", I("memset", one1[:], 1.0), writes=[B_one1])
        op("pool", I("memset", vh[0][:, :, 64:65], 1.0), writes=[B_vh[0]])
        op("pool", I("memset", vh[1][:, :, 64:65], 1.0), writes=[B_vh[1]])

        def load_head(hd):
            sl = hd % 2
            op("sp", I("dma_start", out=qTh[sl][:], in_=qT_d[hd * 64:(hd + 1) * 64, :]),
               reads=[B_qT], writes=[B_qh[sl]], dma_sem=s_hd[sl])
            op("sp", I("dma_start", out=kTh[sl][:], in_=kT_d[hd * 64:(hd + 1) * 64, :]),
               reads=[B_kT], writes=[B_kh[sl]], dma_sem=s_hd[sl])
            op("sp", I("dma_start", out=vh[sl][:, :, 0:64], in_=v_d[hd]),
               reads=[B_v], writes=[B_vh[sl]], dma_sem=s_hd[sl])

        load_head(0)
        NHA = min(8, int(os.environ.get("NHMAX", NH)))
        citems = [(hd, p) for hd in range(NHA) for p in range(NB)]
        NCI = len(citems)

        def hvs(hd):
            hs = hd % 2
            return qTh[hs], kTh[hs], vh[hs], B_qh[hs], B_kh[hs], B_vh[hs], uT[hs], B_uT[hs]

        def jvalid(p):
            return [j for j in range(5) if p - 4 + j >= 0]

        def C0(n):
            hd, p = citems[n]
            q_, k_, v_, Bq, Bk_, Bv, u_, Bu = hvs(hd)
            ps2 = 2 * (n % 2)
            op("pe", [I("matmul", pg[:, ps2 * 512 + j * 128: ps2 * 512 + (j + 1) * 128],
                        k_[:, (p - 4 + j) * 128:(p - 3 + j) * 128], q_[:, p * 128:(p + 1) * 128],
                        start=True, stop=True) for j in jvalid(p)],
               reads=[Bq, Bk_], writes=[BK[ps2], BK[ps2 + 1]])

        def C1(n):
            hd, p = citems[n]
            ps2 = 2 * (n % 2)
            jv = jvalid(p)
            j0 = jv[0]
            pti = n % 3
            nlow = len([j for j in jv if j < 3])
            if nlow > 0:
                op("act", I("activation", PT[pti][:, j0:j0 + nlow, :].rearrange("p a b -> p (a b)"),
                            pg[:, ps2 * 512 + j0 * 128: ps2 * 512 + (j0 + nlow) * 128], AF.Exp,
                            bias=cconst[:, hd:hd + 1]),
                   reads=[BK[ps2], B_layer], writes=[B_PT[pti]])
            jh0 = max(3, j0)
            nh_ = 5 - jh0
            tb = n % 2
            op("dve", I("tensor_tensor", out=tmpb[tb][:, 0:nh_ * 128],
                        in0=pg[:, ps2 * 512 + jh0 * 128: ps2 * 512 + 640],
                        in1=biasT[:, hd, (jh0 - 3) * 128:256], op=ALU.add),
               reads=[BK[ps2], BK[ps2 + 1], B_layer], writes=[B_tmpb[tb]])

        def C2(n):
            hd, p = citems[n]
            jv = jvalid(p)
            j0 = jv[0]
            pti = n % 3
            jh0 = max(3, j0)
            nh_ = 5 - jh0
            tb = n % 2
            op("act", I("activation", PT[pti][:, jh0:5, :].rearrange("p a b -> p (a b)"),
                        tmpb[tb][:, 0:nh_ * 128], AF.Exp),
               reads=[B_tmpb[tb]], writes=[B_PT[pti]])
            if j0 == 0:
                op("pool", I("memset", PT[pti][0:64, 0, 64:128], 0.0), reads=[B_PT[pti]], writes=[B_PT[pti]])

        def C3(n):
            hd, p = citems[n]
            q_, k_, v_, Bq, Bk_, Bv, u_, Bu = hvs(hd)
            jv = jvalid(p)
            pti = n % 3
            ob = 4 + (n % 2)
            op("pe", [I("matmul", bank(ob, 65), PT[pti][:, j, :], v_[:, p - 4 + j, :],
                        start=(j == jv[0]), stop=(j == jv[-1])) for j in jv],
               reads=[B_PT[pti], Bv], writes=[BK[ob]])

        def C4(n):
            ob = 4 + (n % 2)
            rc = rcp[:, (n % 4):(n % 4) + 1]
            op("dve", I("reciprocal", rc, pg[:, ob * 512 + 64: ob * 512 + 65]), reads=[BK[ob]], writes=[B_rcp])
            op("dve", I("tensor_scalar", out=ubf[n % 2][:], in0=bank(ob, 64), scalar1=rc, scalar2=None, op0=ALU.mult),
               reads=[BK[ob], B_rcp], writes=[B_ubf[n % 2]])

        def C5(n):
            us = n % 2
            op("pe", I("transpose", out=ptb[us][0:64, 0:128], in_=ubf[us][:], identity=ident_bf[:]),
               reads=[B_ubf[us], B_const], writes=[B_pt[us]])

        def C6(n):
            hd, p = citems[n]
            q_, k_, v_, Bq, Bk_, Bv, u_, Bu = hvs(hd)
            us = n % 2
            op("act", I("activation", u_[:, p * 128:(p + 1) * 128], ptb[us][0:64, 0:128], AF.Copy),
               reads=[B_pt[us]], writes=[Bu])
            if p == NB - 1:
                op("sp", I("dma_start", out=aT_d[hd], in_=u_[:]), reads=[Bu], writes=[B_aT], dma_sem=s_uT[hd % 2])
            if p == 0 and hd + 1 < NH:
                load_head(hd + 1)

        for t in range(NCI + 6):
            if t < NCI:
                C0(t)
            if 0 <= t - 3 < NCI:
                C3(t - 3)
            if 0 <= t - 5 < NCI:
                C5(t - 5)
            if 0 <= t - 1 < NCI:
                C1(t - 1)
            if 0 <= t - 2 < NCI:
                C2(t - 2)
            if 0 <= t - 4 < NCI:
                C4(t - 4)
            if 0 <= t - 6 < NCI:
                C6(t - 6)

        acc3 = [acc] + [ar.t("acc_x", [128, 512], BF16) for _ in range(2)]
        B_acc3 = [B_acc, Buf("acc1"), Buf("acc2")]
        dmaskR = ar.t("dmaskR", [128, 4, 512], BF16)
        B_dmR = Buf("dmaskR")
        op("pool", [I("tensor_copy", dmaskR[:, r, :], dmask[:, 3 - r, :]) for r in range(4)], reads=[B_const], writes=[B_dmR])
        pso = [ptb[0].bitcast(F32), ptb[1].bitcast(F32)]
        items = []
        for hd in range(8, int(os.environ.get("NHMAX", NH))):
            for T in range(NT):
                npair = 2 * T + 2
                for i in range(npair):
                    items.append((hd, T, i, npair))
        NI = len(items)

        def hv(hd):
            hs = hd % 2
            return qTh[hs], kTh[hs], vh[hs], B_qh[hs], B_kh[hs], B_vh[hs], uT[hs], B_uT[hs]

        def zpair(n):
            b2 = n % 2
            return pg[:, b2 * 1024:(b2 + 1) * 1024], [BK[2 * b2], BK[2 * b2 + 1]]

        PB = pg[:, 2048:3072]
        BPB = [BK[4], BK[5]]

        def kbs_of(n):
            hd, T, i, npair = items[n]
            khi = 2 * npair - 1 - 2 * i
            return khi, khi - 1

        def S0(n):
            hd, T, i, npair = items[n]
            q_, k_, v_, Bq, Bk_, Bv, u_, Bu = hv(hd)
            zp, Bz = zpair(n)
            khi, klo = kbs_of(n)
            qs = q_[:, T * 512:(T + 1) * 512]
            op("pe", [I("matmul", zp[:, 0:512], k_[:, khi * 128:(khi + 1) * 128], qs, start=True, stop=True),
                      I("matmul", zp[:, 512:1024], k_[:, klo * 128:(klo + 1) * 128], qs, start=True, stop=True)],
               reads=[Bq, Bk_], writes=Bz)

        def S1(n):
            zp, Bz = zpair(n)
            op("act", I("activation", e_sb[n % 3][:], zp, AF.Exp), reads=Bz, writes=[B_e[n % 3]])

        def S2(n):
            hd, T, i, npair = items[n]
            khi, klo = kbs_of(n)
            sp_ = sp_sb[n % 4]
            op("act", I("activation", sp_[:], e_sb[n % 3][:], AF.Ln, bias=one1[:, 0:1]),
               reads=[B_e[n % 3], B_one1], writes=[B_sp[n % 4]])
            if khi >= 4 * T:
                r0 = 3 - (khi - 4 * T)
                op("dve", I("tensor_tensor", out=sp_[:], in0=sp_[:], in1=dmaskR[:, r0:r0 + 2, :].rearrange("p a b -> p (a b)"),
                            op=ALU.mult),
                   reads=[B_sp[n % 4], B_dmR], writes=[B_sp[n % 4]])
            if i + 1 < npair:
                if i == 0:
                    op("pool", I("tensor_tensor", out=acc3[(n + 1) % 3][:], in0=sp_[:, 0:512], in1=sp_[:, 512:1024], op=ALU.add),
                       reads=[B_sp[n % 4]], writes=[B_acc3[(n + 1) % 3]])
                else:
                    op("pool", I("tensor_tensor", out=acc3[(n + 1) % 3][:], in0=acc3[n % 3][:], in1=sp_[:, 0:512], op=ALU.add),
                       reads=[B_acc3[n % 3], B_sp[n % 4]], writes=[B_acc3[(n + 1) % 3]])
                    op("pool", I("tensor_tensor", out=acc3[(n + 1) % 3][:], in0=acc3[(n + 1) % 3][:], in1=sp_[:, 512:1024], op=ALU.add),
                       reads=[B_acc3[(n + 1) % 3], B_sp[n % 4]], writes=[B_acc3[(n + 1) % 3]])

        def S3(n):
            hd, T, i, npair = items[n]
            q_, k_, v_, Bq, Bk_, Bv, u_, Bu = hv(hd)
            khi, klo = kbs_of(n)
            qs = q_[:, T * 512:(T + 1) * 512]
            sp_ = sp_sb[n % 4]
            hi, lo = PB[:, 0:512], PB[:, 512:1024]
            ins = [I("matmul", hi, k_[:, khi * 128:(khi + 1) * 128], qs, start=True, stop=False),
                   I("matmul", hi, triN[:], sp_[:, 0:512], start=False, stop=(i == 0))]
            if i > 0:
                ins.append(I("matmul", hi, onesN[:], acc3[n % 3][:], start=False, stop=True))
            ins += [I("matmul", lo, k_[:, klo * 128:(klo + 1) * 128], qs, start=True, stop=False),
                    I("matmul", lo, triN[:], sp_[:, 512:1024], start=False, stop=False),
                    I("matmul", lo, onesN[:], sp_[:, 0:512], start=False, stop=(i == 0))]
            if i > 0:
                ins.append(I("matmul", lo, onesN[:], acc3[n % 3][:], start=False, stop=True))
            rd = [Bq, Bk_, B_sp[n % 4], B_const] + ([B_acc3[n % 3]] if i > 0 else [])
            op("pe", ins, reads=rd, writes=BPB)

        def S4(n):
            hd, T, i, npair = items[n]
            khi, klo = kbs_of(n)
            w_ = W_sb[n % 4]
            op("act", I("activation", w_[:], PB, AF.Exp), reads=BPB, writes=[B_W[n % 4]])
            if khi >= 4 * T:
                r0 = 3 - (khi - 4 * T)
                op("dve", I("tensor_tensor", out=w_[:], in0=w_[:], in1=dmaskR[:, r0:r0 + 2, :].rearrange("p a b -> p (a b)"),
                            op=ALU.mult),
                   reads=[B_W[n % 4], B_dmR], writes=[B_W[n % 4]])

        def S5(n):
            hd, T, i, npair = items[n]
            q_, k_, v_, Bq, Bk_, Bv, u_, Bu = hv(hd)
            khi, klo = kbs_of(n)
            g = (hd - 8) * NT + T
            po = pso[g % 2]
            w_ = W_sb[n % 4]
            op("pe", [I("matmul", po[0:64, 0:512], v_[:, khi, 0:64], w_[:, 0:512], start=(i == 0), stop=False),
                      I("matmul", po[0:64, 0:512], v_[:, klo, 0:64], w_[:, 512:1024], start=False, stop=(i == npair - 1))],
               reads=[B_W[n % 4], Bv], writes=[B_pt[g % 2]])
            if i == npair - 1:
                op("dve", I("tensor_copy", u_[:, T * 512:(T + 1) * 512], po[0:64, 0:512]),
                   reads=[B_pt[g % 2]], writes=[Bu])
                if T == NT - 1:
                    op("sp", I("dma_start", out=aT_d[hd], in_=u_[:]), reads=[Bu], writes=[B_aT], dma_sem=s_uT[hd % 2])

        for t in range(NI + 5):
            if t < NI:
                S0(t)
            if 0 <= t - 1 < NI:
                S1(t - 1)
            if 0 <= t - 4 < NI:
                S4(t - 4)
            if 0 <= t - 3 < NI:
                S3(t - 3)
            if 0 <= t - 2 < NI:
                S2(t - 2)
            if 0 <= t - 5 < NI:
                S5(t - 5)
                hd5, T5, i5, _ = items[t - 5]
                if T5 == 0 and i5 == 0 and hd5 + 1 < NH:
                    load_head(hd5 + 1)

        if stop == "p2":
            P.emit()
            return nc
        P.barrier()
        ar.reset()
        x3s = [ar.t("x3", [128, 4, D], F32) for _ in range(2)]
        B_x3s = [Buf("x3a"), Buf("x3b")]
        aTt = ar.t("aTt", [128, 8, 512], BF16)
        B_aTt = Buf("aTt")
        W3 = dict(junk=ar.t("junk3", [128, D], BF16), B_junk=Buf("junk3"),
                  xhat=[ar.t("xhat3", [128, D], BF16) for _ in range(2)], B_xhat=[Buf("xh3_0"), Buf("xh3_1")],
                  hT=ar.t("hT3", [128, KC, 512], BF16), B_hT=Buf("hT3"),
                  B_stat=Buf("stat3"), eps=ar.t("eps3", [128, 1], F32), B_sc=[Buf("sc3_%d" % i) for i in range(16)])
        stat = ar.t("stat3", [128, 16], F32)
        hT = W3["hT"]; B_hT = W3["B_hT"]; B_stat = W3["B_stat"]; junk = W3["junk"]; B_junk = W3["B_junk"]
        eps3 = W3["eps"]
        op("pool", I("memset", eps3[:], EPS), writes=[B_stat])
        sq = [ar.t("sq", [128, 512], F32) for _ in range(2)]
        B_sq = [Buf("sq0"), Buf("sq1")]
        rstd2 = ar.t("rstd2", [128, 4, 2], F32)
        B_rstd2 = Buf("rstd2")
        wup_sb = [ar.t("wup_sb", [128, KC, 2, 128], BF16) for _ in range(3)]
        B_wup = [Buf("wup%d" % i) for i in range(3)]
        wdn_sb = [ar.t("wdn_sb", [128, 4, 512], BF16) for _ in range(3)]
        B_wdn = [Buf("wdn%d" % i) for i in range(3)]
        aF = ar.t("aF", [128, NJ, 512], BF16)
        B_aF = Buf("aF")
        y0 = [ar.t("y0", [128, 512], F32) for _ in range(6)]
        B_y0 = [Buf("y0_%d" % i) for i in range(6)]
        addv = ar.t("addv", [128, 44, 2], F32)
        tAB = ar.t("tAB", [128, 2, 44], F32)
        B_addv = Buf("addv")
        sg = [ar.t("sg", [128, 512], F32) for _ in range(2)]
        B_sg = [Buf("sg0"), Buf("sg1")]
        tmp3 = [ar.t("tmp3", [128, 512], F32) for _ in range(2)]
        B_tmp3 = [Buf("tmp3_0"), Buf("tmp3_1")]

        wupc = 0
        wdnc = 0
        def load_x3(T):
            src = x_src[T * 512:(T + 1) * 512, :].rearrange("(s p) d -> p s d", p=128)
            op("sp", I("dma_start", out=x3s[T % 2][:], in_=src), reads=[B_xsrc], writes=[B_x3s[T % 2]], dma_sem=s_x3l[T % 2])

        def load_aTt(T):
            for two in range(2):
                srca = aT_d[:, :, T * 512:(T + 1) * 512].rearrange("(hp two) d t -> two d hp t", two=2)[two]
                op("sp", I("dma_start", out=aTt[two * 64:(two + 1) * 64, :, :], in_=srca),
                   reads=[B_aT], writes=[B_aTt], dma_sem=s_aTt)

        load_x3(0)
        load_aTt(0)
        for T in range(NT):
            x3 = x3s[T % 2]
            B_x3 = B_x3s[T % 2]
            if T + 1 < NT:
                load_x3(T + 1)
            for g in range(2):
                for hh in range(4):
                    hp = g * 4 + hh
                    sl = hp % 2
                    op("act", I("activation", sq[sl][:], aTt[:, hp, :], AF.Square), reads=[B_aTt], writes=[B_sq[sl]])
                    op("pe", [I("matmul", pg[:, s * 512 + g: s * 512 + g + 1], sq[sl][:, s * 128:(s + 1) * 128],
                                ones_col[:, :], start=(hh == 0), stop=(hh == 3)) for s in range(4)],
                       reads=[B_sq[sl], B_const], writes=[BK[0], BK[1], BK[2], BK[3]])
            op("act", [I("activation", rstd2[:, s, :], pg[:, s * 512: s * 512 + 2], AF.Ln, bias=eps3[:, 0:1], scale=1.0 / 512)
                       for s in range(4)],
               reads=[BK[0], BK[1], BK[2], BK[3], B_stat], writes=[B_rstd2])
            op("act", I("activation", rstd2[:].rearrange("p a b -> p (a b)"), rstd2[:].rearrange("p a b -> p (a b)"),
                        AF.Exp, scale=-0.5), reads=[B_rstd2], writes=[B_rstd2])
            for s in range(4):
                for n in range(2):
                    bA = 2 * ((s * 2 + n) % 2)
                    bB = bA + 1
                    ins = [I("matmul", bank(bA), aTt[:, hp, s * 128:(s + 1) * 128], wout2[:, hp, n * 512:(n + 1) * 512],
                             start=(hp == 0), stop=(hp == 3)) for hp in range(4)]
                    ins += [I("matmul", bank(bB), aTt[:, hp, s * 128:(s + 1) * 128], wout2[:, hp, n * 512:(n + 1) * 512],
                              start=(hp == 4), stop=(hp == 7)) for hp in range(4, 8)]
                    op("pe", ins, reads=[B_aTt, B_wout2], writes=[BK[bA], BK[bB]])
                    xv = x3[:, s, n * 512:(n + 1) * 512]
                    op("dve", I("scalar_tensor_tensor", out=xv, in0=bank(bA), scalar=rstd2[:, s, 0:1], in1=xv,
                                op0=ALU.mult, op1=ALU.add), reads=[BK[bA], B_rstd2, B_x3], writes=[B_x3])
                    op("dve", I("scalar_tensor_tensor", out=xv, in0=bank(bB), scalar=rstd2[:, s, 1:2], in1=xv,
                                op0=ALU.mult, op1=ALU.add), reads=[BK[bB], B_rstd2, B_x3], writes=[B_x3])
                norm_to_hT(x3[:, s, :], B_x3, s, 2, 3, stat[:, s:s + 1], W3, ci=s, part="a")
                if s >= 1:
                    norm_to_hT(x3[:, s - 1, :], B_x3, s - 1, 2, 3, stat[:, s - 1:s], W3, ci=s - 1, part="b")
            if T + 1 < NT:
                load_aTt(T + 1)
            norm_to_hT(x3[:, 3, :], B_x3, 3, 2, 3, stat[:, 3:4], W3, ci=3, part="b")
            cold = carry[:, (T + 1) % 2, :, :]
            cnew = carry[:, T % 2, :, :]
            Bcold = B_carry[(T + 1) % 2]
            Bcnew = B_carry[T % 2]
            op("dve", [I("tensor_tensor", out=tAB[:, 0, :], in0=cold[:, :, 0], in1=cwT[:, 0:44], op=ALU.mult),
                       I("tensor_tensor", out=tAB[:, 1, :], in0=cold[:, :, 1], in1=cwT[:, 44:88], op=ALU.mult),
                       I("tensor_tensor", out=addv[:, :, 1], in0=cold[:, :, 1], in1=cwT[:, 0:44], op=ALU.mult)],
               reads=[Bcold, B_layer], writes=[B_addv])
            op("dve", I("tensor_tensor", out=addv[:, :, 0], in0=tAB[:, 0, :], in1=tAB[:, 1, :], op=ALU.add),
               reads=[B_addv], writes=[B_addv])
            def ffn_fin(jj):
                u0 = (2 * jj) % 6
                u1 = (2 * jj + 1) % 6
                op("act", I("activation", sg[jj % 2][:], y0[u0][:], AF.Silu), reads=[B_y0[u0]], writes=[B_sg[jj % 2]])
                op("pool", I("tensor_tensor", out=aF[:, jj, :], in0=y0[u1][:], in1=sg[jj % 2][:], op=ALU.mult),
                   reads=[B_y0[u1], B_sg[jj % 2]], writes=[B_aF])

            for j in range(NJ):
                wsl = wupc % 3
                wupc += 1
                if not os.environ.get("NOWLOAD"):
                    op("sp", I("dma_start", out=wup_sb[wsl][:], in_=wupb_d[l, j]),
                       reads=[B_wupb[l]], writes=[B_wup[wsl]], dma_sem=s_wup[wsl])
                for g in range(2):
                    bk = 2 * (j % 3) + g
                    ui = (2 * j + g) % 6
                    jg = g * NJ + j
                    op("pe", [I("matmul", bank(bk), wup_sb[wsl][:, kc, g, :], hT[:, kc, :], start=(kc == 0), stop=(kc == KC - 1))
                              for kc in range(KC)],
                       reads=[B_wup[wsl], B_hT], writes=[BK[bk]])
                    filler()
                    op("act", I("activation", y0[ui][:], bank(bk), AF.Identity, bias=cbT[:, jg:jg + 1],
                                scale=cwT[:, 88 + jg:88 + jg + 1]),
                       reads=[BK[bk], B_layer], writes=[B_y0[ui]])
                    op("act", I("activation", cnew[:, jg, :], pg[:, bk * 512 + 510: bk * 512 + 512], AF.Copy),
                       reads=[BK[bk]], writes=[Bcnew])
                    op("dve", I("scalar_tensor_tensor", out=y0[ui][:, 1:512], in0=pg[:, bk * 512: bk * 512 + 511],
                                scalar=cwT[:, 44 + jg:44 + jg + 1], in1=y0[ui][:, 1:512], op0=ALU.mult, op1=ALU.add),
                       reads=[BK[bk], B_y0[ui], B_layer], writes=[B_y0[ui]])
                    op("dve", I("scalar_tensor_tensor", out=y0[ui][:, 2:512], in0=pg[:, bk * 512: bk * 512 + 510],
                                scalar=cwT[:, jg:jg + 1], in1=y0[ui][:, 2:512], op0=ALU.mult, op1=ALU.add),
                       reads=[BK[bk], B_y0[ui], B_layer], writes=[B_y0[ui]])
                    op("pool", I("tensor_tensor", out=y0[ui][:, 0:2], in0=y0[ui][:, 0:2], in1=addv[:, jg, :], op=ALU.add),
                       reads=[B_addv, B_y0[ui]], writes=[B_y0[ui]])
                    if g == 1 and j >= 1:
                        ffn_fin(j - 1)
            ffn_fin(NJ - 1)
            for n in range(2):
                for jg0 in range(0, NJ, 4):
                    nj = min(4, NJ - jg0)
                    dsl = wdnc % 3
                    wdnc += 1
                    src = wdnb_d[l, jg0 * 128:(jg0 + nj) * 128, n * 512:(n + 1) * 512].rearrange("(j p) c -> p j c", p=128)
                    if not os.environ.get("NOWLOAD"):
                        op("sp", I("dma_start", out=wdn_sb[dsl][:, 0:nj, :], in_=src),
                           reads=[B_wdnb[l]], writes=[B_wdn[dsl]], dma_sem=s_wdn[dsl])
                    ins = []
                    for s in range(4):
                        for jj in range(nj):
                            j = jg0 + jj
                            ins.append(I("matmul", bank(2 + s), aF[:, j, s * 128:(s + 1) * 128], wdn_sb[dsl][:, jj, :],
                                         start=(j == 0), stop=(j == NJ - 1)))
                    op("pe", ins, reads=[B_aF, B_wdn[dsl]], writes=[BK[2], BK[3], BK[4], BK[5]])
                    filler()
                for s in range(4):
                    ts = s % 2
                    xv = x3[:, s, n * 512:(n + 1) * 512]
                    op("dve", I("tensor_tensor", out=tmp3[ts][:], in0=bank(2 + s), in1=gate_b[:, 1, n * 512:(n + 1) * 512],
                                op=ALU.mult),
                       reads=[BK[2 + s], B_layer], writes=[B_tmp3[ts]])
                    op("pool", I("tensor_tensor", out=xv, in0=xv, in1=tmp3[ts][:], op=ALU.add),
                       reads=[B_tmp3[ts], B_x3], writes=[B_x3])
            if last:
                for s in range(4):
                    sc = stat[:, 8 + s:9 + s]
                    op("act", I("activation", junk[:], x3[:, s, :], AF.Square, accum_out=sc),
                       reads=[B_x3], writes=[B_junk, B_stat])
                    op("act", I("activation", sc, sc, AF.Ln, bias=eps3[:, 0:1], scale=1.0 / D), reads=[B_stat], writes=[B_stat])
                    op("act", I("activation", sc, sc, AF.Exp, scale=-0.5), reads=[B_stat], writes=[B_stat])
                    op("dve", I("scalar_tensor_tensor", out=x3[:, s, :], in0=x3[:, s, :], scalar=sc, in1=fg_b[:],
                                op0=ALU.mult, op1=ALU.mult),
                       reads=[B_x3, B_stat, B_const], writes=[B_x3])
            dst = x_dst[T * 512:(T + 1) * 512, :].rearrange("(s p) d -> p s d", p=128)
            op("sp", I("dma_start", out=dst, in_=x3[:]), reads=[B_x3], writes=[B_xdst], dma_sem=s_x3o)

    P.emit()
    return nc


_CACHE = {}


def make_in_maps(inputs, S, DEPTH, ncores):
    f = lambda a: np.ascontiguousarray(np.asarray(a, dtype=np.float32))
    x = f(inputs["x"]); c = f(inputs["c"])
    g = np.concatenate([f(inputs["g_a"]).reshape(DEPTH, 8, 64), f(inputs["g_b"]).reshape(DEPTH, 8, 64)], axis=1)
    shared = dict(
        w_ada=f(inputs["w_ada"]), b_ada=f(inputs["b_ada"]), w_in=f(inputs["w_in"]),
        rel_bias=f(inputs["rel_bias"]), g=np.ascontiguousarray(g), w_out=f(inputs["w_out"]),
        w_up=f(inputs["w_up"]), conv_w=f(inputs["conv_w"]).reshape(DEPTH, 132, 128),
        conv_b=f(inputs["conv_b"]).reshape(DEPTH, 44, 128), w_down=f(inputs["w_down"]),
        final_g=f(inputs["final_g"]).reshape(1, D),
    )
    maps = []
    for b in range(ncores):
        m = dict(shared)
        m["x"] = np.ascontiguousarray(x[b])
        m["c"] = np.ascontiguousarray(c[b].reshape(8, 128))
        maps.append(m)
    return maps


def kernel(x, c, w_ada, b_ada, w_in, rel_bias, g_a, g_b, w_out, w_up, conv_w, conv_b, w_down, final_g):
    inputs = dict(x=x, c=c, w_ada=w_ada, b_ada=b_ada, w_in=w_in, rel_bias=rel_bias, g_a=g_a, g_b=g_b,
                  w_out=w_out, w_up=w_up, conv_w=conv_w, conv_b=conv_b, w_down=w_down, final_g=final_g)
    Bn, S, _ = np.asarray(x).shape
    DEPTH = np.asarray(w_in).shape[0]
    key = (S, DEPTH)
    if key not in _CACHE:
        _CACHE[key] = build(S, DEPTH)
    nc = _CACHE[key]
    maps = make_in_maps(inputs, S, DEPTH, Bn)
    res = run_bass_kernel_spmd(nc, maps, core_ids=list(range(Bn)))
    out = np.stack([np.asarray(r["out"]) for r in res.results], axis=0)
    return out.astype(np.float32)
```

```python
import numpy as np
import concourse.bass as bass
import concourse.mybir as mybir
from concourse.bass_utils import run_bass_kernel_spmd

F32 = mybir.dt.float32
BF16 = mybir.dt.bfloat16
AF = mybir.ActivationFunctionType
ALU = mybir.AluOpType

D = 1024
KC = 8
NH = 16
DH = 64
DFF = 2816
NJ = 22
EPS = 1e-6
NEG = -30000.0


import os as _os
STRICT = _os.environ.get("KSTRICT", "1") != "0"


class Buf:
    __slots__ = ("name", "writers", "readers", "excl")

    def __init__(self, name, excl=False):
        self.name = name
        self.writers = {}
        self.readers = {}
        self.excl = excl


class SemObj:
    __slots__ = ("h", "count", "name", "is_dma")

    def __init__(self, h, name):
        self.h = h
        self.count = 0
        self.name = name
        self.is_dma = False


class Eng:
    def __init__(self, name, handle, sem):
        self.name = name
        self.handle = handle
        self.sem = sem
        self.ops = []
        self.seen = {}


class Prog:
    def __init__(self, nc):
        self.nc = nc
        self.engs = {}
        self.sems = []

    def new_sem(self, name):
        s = SemObj(self.nc.alloc_semaphore(name=name), name)
        self.sems.append(s)
        return s

    def add_engine(self, name, handle):
        e = Eng(name, handle, self.new_sem("e_" + name))
        self.engs[name] = e
        return e

    def op(self, eng, fn, reads=(), writes=(), dma_sem=None, extra=()):
        if isinstance(fn, tuple):
            fn = [fn]
        ex_r = [b for b in reads if b.excl]
        if ex_r:
            reads = [b for b in reads if not b.excl]
            writes = list(writes) + [b for b in ex_r if b not in writes]
        e = self.engs[eng]
        is_dma = dma_sem is not None
        if not is_dma and e.sem.count >= 60000:
            e.sem = self.new_sem("e_%s_%d" % (eng, len(self.sems)))
        waits = {}

        def addw(s, v):
            if waits.get(s, -1) < v:
                waits[s] = v

        for b in reads:
            for s, v in b.writers.items():
                addw(s, v)
        for b in writes:
            skip_same = (not STRICT) or (eng == "pe" and b.excl)
            for s, v in b.writers.items():
                if s is e.sem and not is_dma and skip_same:
                    continue
                addw(s, v)
            for s, v in b.readers.items():
                if s is e.sem and not is_dma and skip_same:
                    continue
                addw(s, v)
        for s, v in extra:
            addw(s, v)
        wl = []
        for s, v in list(waits.items()):
            if s.is_dma:
                v = s.count
            if e.seen.get(s, -1) >= v:
                continue
            e.seen[s] = v
            wl.append((s, v))
        if is_dma:
            dma_sem.is_dma = True
            dma_sem.count += 16
            tok = (dma_sem, dma_sem.count)
            e.ops.append((wl, fn, dma_sem, 16))
        else:
            e.sem.count += 1
            tok = (e.sem, e.sem.count)
            e.ops.append((wl, fn, e.sem, 1))
        for b in reads:
            if b.readers.get(tok[0], -1) < tok[1]:
                b.readers[tok[0]] = tok[1]
        for b in writes:
            if b.writers.get(tok[0], -1) < tok[1]:
                b.writers[tok[0]] = tok[1]
        return tok

    def barrier(self):
        for e in self.engs.values():
            wl = []
            for s in self.sems:
                if s.count > 0 and e.seen.get(s, -1) < s.count:
                    e.seen[s] = s.count
                    wl.append((s, s.count))
            if wl:
                e.ops.append((wl, None, None, 0))

    def emit(self):
        nc = self.nc
        self.barrier()
        with nc.Block() as block:
            def mk(e):
                def body(h):
                    for wl, fn, isem, ival in e.ops:
                        for s, v in wl:
                            h.wait_ge(s.h, v)
                        if fn is not None:
                            ins = None
                            for (m, a, k) in fn:
                                ins = getattr(h, m)(*a, **k)
                            ins.then_inc(isem.h, ival)
                return body
            decs = {"pe": block.tensor, "act": block.scalar, "dve": block.vector,
                    "pool": block.gpsimd, "sp": block.sync}
            for name, e in self.engs.items():
                decs[name](mk(e))


def I(m, *a, **k):
    return (m, a, k)


class Arena:
    def __init__(self, nc, base, limit):
        self.nc = nc
        self.base = base
        self.top = base
        self.limit = limit
        self.n = 0

    def t(self, name, shape, dtype):
        esz = 4 if dtype == F32 else 2
        n = esz
        for s in shape[1:]:
            n *= s
        off = (self.top + 63) // 64 * 64
        assert off + n <= self.limit, ("SBUF overflow", name, off, n, self.limit)
        self.top = off + n
        self.n += 1
        return self.nc.alloc_sbuf_tensor_at("%s_%d" % (name, self.n), list(shape), dtype, offset=off)

    def reset(self):
        self.top = self.base


def build(S, DEPTH, dbg=False, stop=None):
    import os
    nc = bass.Bass("TRN2", target_bir_lowering=False)
    NT = S // 512
    NB = S // 128
    skind = "ExternalOutput" if dbg else "Internal"

    def din(name, shape, dt=F32):
        return nc.dram_tensor(name, list(shape), dt, kind="ExternalInput")

    x_d = din("x", [S, D])
    c_d = din("c", [8, 128])
    wada_d = din("w_ada", [DEPTH, D, 6 * D])
    bada_d = din("b_ada", [DEPTH, 6 * D])
    win_d = din("w_in", [DEPTH, D, 3 * D])
    relb_d = din("rel_bias", [DEPTH, 8, 257])
    g_d = din("g", [DEPTH, 16, 64])
    wout_d = din("w_out", [DEPTH, D, D])
    wup_d = din("w_up", [DEPTH, D, 2 * DFF])
    convw_d = din("conv_w", [DEPTH, 132, 128])
    convb_d = din("conv_b", [DEPTH, 44, 128])
    wdn_d = din("w_down", [DEPTH, DFF, D])
    fg_d = din("final_g", [1, D])
    out_d = nc.dram_tensor("out", [S, D], F32, kind="ExternalOutput")

    def dscr(name, shape, dt):
        return nc.dram_tensor(name, list(shape), dt, kind=skind)

    winb_d = nc.dram_tensor("win_b", [DEPTH, D, 3 * D], BF16, kind="Internal")
    wupb_d = nc.dram_tensor("wup_b", [DEPTH, NJ, 128, KC, 2, 128], BF16, kind="Internal")
    wdnb_d = nc.dram_tensor("wdn_b", [DEPTH, DFF, D], BF16, kind="Internal")
    ext_d = nc.dram_tensor("ext_s", [8, 640], F32, kind="Internal")
    qT_d = dscr("qT_s", [NH * DH, S], BF16)
    kT_d = dscr("kT_s", [NH * DH, S], BF16)
    v_d = dscr("v_s", [NH, 128, NB, DH], BF16)
    aT_d = dscr("aT_s", [NH, DH, S], BF16)
    xs_d = dscr("xs_s", [S, D], F32)

    P = Prog(nc)
    for n, h in (("pe", nc.tensor), ("act", nc.scalar), ("dve", nc.vector),
                 ("pool", nc.gpsimd), ("sp", nc.sync)):
        P.add_engine(n, h)
    op = P.op

    SB0 = 16640
    PERS = SB0 + 67584
    pa = Arena(nc, SB0, PERS)
    ident_bf = pa.t("ident_bf", [128, 128], BF16)
    ident_f = pa.t("ident_f", [128, 128], F32)
    Jf = pa.t("Jf", [128, 128], F32)
    triN = pa.t("triN", [128, 128], BF16)
    onesN = pa.t("onesN", [128, 128], BF16)
    dmask = pa.t("dmask", [128, 4, 512], BF16)
    ones_col = pa.t("ones_col", [128, 1], F32)
    ones_row = pa.t("ones_row", [1, 128], F32)
    cactT = pa.t("cactT", [128, 8], F32)
    cact_bc = pa.t("cact_bc", [128, 8, 128], F32)
    shsc = pa.t("shsc", [128, 4, 8], F32)
    gate_b = pa.t("gate_b", [128, 2, 1024], F32)
    bT = pa.t("bT", [128, 48], F32)
    gT = pa.t("gT", [128, 8], F32)
    cwT = pa.t("cwT", [128, 132], F32)
    cbT = pa.t("cbT", [128, 44], F32)
    biasT = pa.t("biasT", [128, 8, 256], F32)
    cconst = pa.t("cconst", [128, 8], F32)
    carry = pa.t("carry", [128, 2, 44, 2], F32)
    fg_b = pa.t("fg_b", [128, 1024], F32)
    wout2 = pa.t("wout2", [128, 8, 1024], BF16)
    B_const = Buf("const")
    B_layer = Buf("layer")
    B_wout2 = Buf("wout2")
    B_carry = [Buf("carry0"), Buf("carry1")]

    ar = Arena(nc, PERS, 229000)

    pg = nc.alloc_psum_tensor("pg", [128, 6 * 512], F32)
    ptb = [nc.alloc_psum_tensor("ptb0", [128, 1024], BF16), nc.alloc_psum_tensor("ptb1", [128, 1024], BF16)]
    BK = [Buf("bank%d" % i, excl=True) for i in range(6)]
    B_pt = [Buf("pt0", excl=True), Buf("pt1", excl=True)]

    def bank(i, n=512):
        return pg[:, i * 512:i * 512 + n]

    NFILL = int(os.environ.get("NFILL", "0"))
    fill_out = ptb[1].bitcast(F32)

    def filler(n=None):
        n = NFILL if n is None else n
        if n <= 0:
            return
        op("pe", [I("matmul", fill_out[:, 0:512], ident_bf[:], dmask[:, 0, :], start=True, stop=True) for _ in range(n)],
           reads=[B_const], writes=[B_pt[1]])

    B_x = Buf("x_in")
    B_winb = [Buf("winb%d" % l) for l in range(DEPTH)]
    B_wupb = [Buf("wupb%d" % l) for l in range(DEPTH)]
    B_wdnb = [Buf("wdnb%d" % l) for l in range(DEPTH)]
    B_ext = Buf("ext")
    B_qT = Buf("qT"); B_kT = Buf("kT"); B_v = Buf("v"); B_aT = Buf("aT"); B_xs = Buf("xs"); B_out = Buf("out")

    cast_sems = [P.new_sem("cast%d" % i) for i in range(4)]
    cast_n = [0]

    def cast_op(dst, src, B, sem_unused=None):
        sem = cast_sems[cast_n[0] % 4]
        cast_n[0] += 1
        ex = [(sem, sem.count)] if sem.count > 0 else []
        tok = op("pool", I("dma_start", out=dst, in_=src), dma_sem=sem, extra=ex)
        B.writers[tok[0]] = tok[1]

    def cast_win(l):
        for kc in range(KC):
            src = win_d[l, kc * 128:(kc + 1) * 128, :].rearrange("p (a b) -> p a b", b=1024)
            dst = winb_d[l, kc * 128:(kc + 1) * 128, :].rearrange("p (a b) -> p a b", b=1024)
            cast_op(dst, src, B_winb[l])

    def cast_ffn(l):
        for j in range(NJ if not os.environ.get("NOWUP") else 0):
            for g in range(2):
                src = wup_d[l, :, g * DFF + j * 128: g * DFF + (j + 1) * 128].rearrange("(kc p) c -> p kc c", p=128)
                dst = wupb_d[l, j, :, :, g, :]
                cast_op(dst, src, B_wupb[l])
        for j in range(NJ if not os.environ.get("NOWDN") else 0):
            src = wdn_d[l, j * 128:(j + 1) * 128, :]
            dst = wdnb_d[l, j * 128:(j + 1) * 128, :]
            cast_op(dst, src, B_wdnb[l])

    op("pool", [I("memset", ident_bf[:], 0.0), I("memset", ident_f[:], 0.0), I("memset", Jf[:], 0.0),
                I("memset", triN[:], -1.0), I("memset", onesN[:], -1.0), I("memset", ones_col[:], 1.0),
                I("memset", ones_row[:], 1.0), I("memset", dmask[:], 1.0)], writes=[B_const])
    ins = [
        I("affine_select", out=ident_bf[:], in_=ident_bf[:], pattern=[[-1, 128]], compare_op=ALU.not_equal,
          fill=1.0, base=0, channel_multiplier=1),
        I("affine_select", out=ident_f[:], in_=ident_f[:], pattern=[[-1, 128]], compare_op=ALU.not_equal,
          fill=1.0, base=0, channel_multiplier=1),
        I("affine_select", out=Jf[:], in_=Jf[:], pattern=[[1, 128]], compare_op=ALU.not_equal,
          fill=1.0, base=-127, channel_multiplier=1),
        I("affine_select", out=triN[:], in_=triN[:], pattern=[[-1, 128]], compare_op=ALU.is_ge,
          fill=0.0, base=0, channel_multiplier=1),
    ]
    for i in range(4):
        ins.append(I("affine_select", out=dmask[:, i, :], in_=dmask[:, i, :], pattern=[[1, 512]],
                     compare_op=ALU.is_gt, fill=0.0, base=-128 * i, channel_multiplier=-1))
    op("pool", ins, reads=[B_const], writes=[B_const])
    cast_win(0)

    ar.reset()
    c_sb = ar.t("c_sb", [8, 128], F32)
    c_act = ar.t("c_act", [8, 128], F32)
    fg_row = ar.t("fg_row", [1, 1024], F32)
    B_c = Buf("c_sb"); B_ca = Buf("c_act"); B_fgr = Buf("fg_row")
    s_misc = P.new_sem("misc")
    op("sp", I("dma_start", out=c_sb[:], in_=c_d[:, :]), writes=[B_c], dma_sem=s_misc)
    op("sp", I("dma_start", out=fg_row[:], in_=fg_d[:, :]), writes=[B_fgr], dma_sem=s_misc)
    op("act", I("activation", c_act[:], c_sb[:], AF.Silu), reads=[B_c], writes=[B_ca])
    op("pe", I("transpose", out=bank(0, 8), in_=c_act[:], identity=ident_f[0:8, 0:8]),
       reads=[B_ca, B_const], writes=[BK[0]])
    op("dve", I("tensor_copy", cactT[:], bank(0, 8)), reads=[BK[0]], writes=[B_const])
    op("dve", [I("tensor_copy", cact_bc[:, kc, :], cactT[:, kc:kc + 1].to_broadcast([128, 128])) for kc in range(KC)],
       reads=[B_const], writes=[B_const])
    op("pe", [I("matmul", bank(1), ones_row[0:1, :], fg_row[0:1, 0:512], start=True, stop=True),
              I("matmul", bank(2), ones_row[0:1, :], fg_row[0:1, 512:1024], start=True, stop=True)],
       reads=[B_fgr, B_const], writes=[BK[1], BK[2]])
    op("dve", [I("tensor_copy", fg_b[:, 0:512], bank(1)), I("tensor_copy", fg_b[:, 512:1024], bank(2))],
       reads=[BK[1], BK[2]], writes=[B_const])

    if stop == "p0":
        P.emit()
        return nc
    s_wa = [P.new_sem("wa%d" % i) for i in range(4)]
    s_wo = [P.new_sem("wo0"), P.new_sem("wo1")]
    s_small = P.new_sem("small")
    s_win = P.new_sem("win")
    s_xt = [P.new_sem("xt0"), P.new_sem("xt1")]
    s_qk = [P.new_sem("qk%d" % i) for i in range(4)]
    s_vst = [P.new_sem("vst%d" % i) for i in range(4)]
    s_hd = [P.new_sem("hd0"), P.new_sem("hd1")]
    s_uT = [P.new_sem("uT0"), P.new_sem("uT1")]
    s_x3l = [P.new_sem("x3a"), P.new_sem("x3b")]
    s_x3o = P.new_sem("x3o")
    s_aTt = P.new_sem("aTt")
    s_wup = [P.new_sem("wup%d" % i) for i in range(3)]
    s_wdn = [P.new_sem("wdn%d" % i) for i in range(3)]

    def norm_to_hT(xs_, B_xtile, s, vi_shift, vi_scale, sc, W, part="ab", hT=None, B_hT=None, ci=0):
        xh = W["xhat"][s % 2]
        B_xh = W["B_xhat"][s % 2]
        hT = W["hT"] if hT is None else hT
        B_hT = W["B_hT"] if B_hT is None else B_hT
        B_sc = W["B_sc"][ci]
        if "a" in part:
            op("act", I("activation", W["junk"][:], xs_, AF.Square, accum_out=sc),
               reads=[B_xtile], writes=[W["B_junk"], B_sc])
            op("act", I("activation", sc, sc, AF.Ln, bias=W["eps"][:, 0:1], scale=1.0 / D), reads=[B_sc, W["B_stat"]], writes=[B_sc])
            op("act", I("activation", sc, sc, AF.Exp, scale=-0.5), reads=[B_sc], writes=[B_sc])
            op("dve", I("tensor_scalar", out=xh[:], in0=xs_, scalar1=sc, scalar2=None, op0=ALU.mult),
               reads=[B_xtile, B_sc], writes=[B_xh])
        if "b" in part:
            for half in range(2):
                op("pe", [I("transpose", out=ptb[half][:, k4 * 128:(k4 + 1) * 128],
                            in_=xh[:, (half * 4 + k4) * 128:(half * 4 + k4 + 1) * 128], identity=ident_bf[:])
                          for k4 in range(4)],
                   reads=[B_xh, B_const], writes=[B_pt[half]])
                for k4 in range(4):
                    kc = half * 4 + k4
                    op("dve", I("tensor_scalar", out=hT[:, kc, s * 128:(s + 1) * 128], in0=ptb[half][:, k4 * 128:(k4 + 1) * 128],
                                scalar1=shsc[:, vi_scale, kc:kc + 1], scalar2=shsc[:, vi_shift, kc:kc + 1],
                                op0=ALU.mult, op1=ALU.add),
                       reads=[B_pt[half], B_layer], writes=[B_hT])

    for l in range(DEPTH):
        x_src = x_d if l == 0 else xs_d
        B_xsrc = B_x if l == 0 else B_xs
        last = (l == DEPTH - 1)
        x_dst = out_d if last else xs_d
        B_xdst = B_out if last else B_xs

        P.barrier()
        ar.reset()
        wa = [ar.t("wa", [128, KC, 512], F32) for _ in range(4)]
        B_wa = [Buf("wa%d" % i) for i in range(4)]
        brow = ar.t("brow", [1, 2, 1024], F32)
        b48 = ar.t("b48", [48, 128], F32)
        g16 = ar.t("g16", [8, 128], F32)
        cw1 = ar.t("cw1", [128, 128], F32)
        cw2 = ar.t("cw2", [4, 128], F32)
        cb44 = ar.t("cb44", [44, 128], F32)
        ext_sb = ar.t("ext_sb", [1, 8, 640], F32)
        hank = ar.t("hank", [128, 8, 2, 128], F32)
        wo_st = [ar.t("wo_st", [128, 1024], F32) for _ in range(2)]
        B_wo = [Buf("wo0"), Buf("wo1")]
        B_small = Buf("small")
        B_exts = Buf("ext_sb"); B_hank = Buf("hank")

        op("sp", I("dma_start", out=b48[:], in_=bada_d[l].rearrange("(a b) -> a b", b=128)),
           writes=[B_small], dma_sem=s_small)
        op("sp", I("dma_start", out=brow[0:1, 0, :], in_=bada_d[l:l + 1, 2 * D:3 * D]), writes=[B_small], dma_sem=s_small)
        op("sp", I("dma_start", out=brow[0:1, 1, :], in_=bada_d[l:l + 1, 5 * D:6 * D]), writes=[B_small], dma_sem=s_small)
        op("sp", I("dma_start", out=g16[:], in_=g_d[l].rearrange("(a two) d -> a (two d)", two=2)), writes=[B_small], dma_sem=s_small)
        op("sp", I("dma_start", out=cw1[:], in_=convw_d[l, 0:128, :]), writes=[B_small], dma_sem=s_small)
        op("sp", I("dma_start", out=cw2[:], in_=convw_d[l, 128:132, :]), writes=[B_small], dma_sem=s_small)
        op("sp", I("dma_start", out=cb44[:], in_=convb_d[l]), writes=[B_small], dma_sem=s_small)
        op("sp", I("dma_start", out=ext_sb[0:1, :, 0:256], in_=relb_d[l:l + 1, :, 1:257]), writes=[B_exts], dma_sem=s_small)

        op("pe", [I("transpose", out=bank(0, 48), in_=b48[:], identity=ident_f[0:48, 0:48]),
                  I("transpose", out=pg[:, 512:512 + 8], in_=g16[:], identity=ident_f[0:8, 0:8]),
                  I("transpose", out=bank(2, 128), in_=cw1[:], identity=ident_f[:]),
                  I("transpose", out=pg[:, 2 * 512 + 128:2 * 512 + 132], in_=cw2[:], identity=ident_f[0:4, 0:4]),
                  I("transpose", out=bank(3, 44), in_=cb44[:], identity=ident_f[0:44, 0:44])],
           reads=[B_small, B_const], writes=[BK[0], BK[1], BK[2], BK[3]])
        op("dve", [I("tensor_copy", bT[:], bank(0, 48)),
                   I("tensor_copy", gT[:], pg[:, 512:512 + 8]),
                   I("tensor_copy", cwT[:], bank(2, 132)),
                   I("tensor_copy", cbT[:], bank(3, 44))],
           reads=[BK[0], BK[1], BK[2], BK[3]], writes=[B_layer])
        op("pool", I("memset", carry[:], 0.0), writes=[B_carry[0], B_carry[1]])

        op("dve", I("tensor_copy", ext_sb[0:1, :, 256:640], ext_sb[0:1, :, 255:256].to_broadcast([1, 8, 384])),
           reads=[B_exts], writes=[B_exts])
        op("sp", I("dma_start", out=ext_d[:, :], in_=ext_sb[0:1, :, :]), reads=[B_exts], writes=[B_ext], dma_sem=s_small)
        for hh in range(8):
            for bi, base in ((0, 128), (1, 0)):
                src = bass.AP(ext_d, hh * 640 + base, [[1, 128], [1, 128]])
                op("sp", I("dma_start", out=hank[:, hh, bi, :], in_=src), reads=[B_ext], writes=[B_hank], dma_sem=s_small)
        srcc = bass.AP(ext_d, 300, [[1, 128], [640, 8], [1, 1]])
        op("sp", I("dma_start", out=cconst[:].rearrange("p (a b) -> p a b", b=1), in_=srcc, allow_slow_non_contiguous=True),
           reads=[B_ext], writes=[B_layer], dma_sem=s_small)
        for hh in range(8):
            bk = 4 + (hh % 2)
            op("pe", I("matmul", bank(bk, 256), Jf[:], hank[:, hh, :, :].rearrange("p a b -> p (a b)"), start=True, stop=True),
               reads=[B_hank, B_const], writes=[BK[bk]])
            op("dve", I("tensor_copy", biasT[:, hh, :], bank(bk, 256)), reads=[BK[bk]], writes=[B_layer])
        op("pool", I("memset", biasT[64:128, :, 128:192], NEG), writes=[B_layer])

        for cb in range(12):
            which, half = cb // 2, cb % 2
            sl = cb % 4
            src = wada_d[l, :, cb * 512:(cb + 1) * 512].rearrange("(kc p) c -> p kc c", p=128)
            op("sp", I("dma_start", out=wa[sl][:], in_=src), writes=[B_wa[sl]], dma_sem=s_wa[sl])
            if which in (2, 5):
                gi = 0 if which == 2 else 1
                ins = [I("matmul", bank(5), cact_bc[:, kc, :], wa[sl][:, kc, :], start=(kc == 0), stop=False)
                       for kc in range(KC)]
                ins.append(I("matmul", bank(5), ones_row[0:1, :], brow[0:1, gi, half * 512:(half + 1) * 512],
                             start=False, stop=True))
                op("pe", ins, reads=[B_wa[sl], B_const, B_small], writes=[BK[5]])
                op("dve", I("tensor_copy", gate_b[:, gi, half * 512:(half + 1) * 512], bank(5)),
                   reads=[BK[5]], writes=[B_layer])
            else:
                vi = {0: 0, 1: 1, 3: 2, 4: 3}[which]
                ins = []
                for ch in range(4):
                    for kc in range(KC):
                        ins.append(I("matmul", pg[:, 3 * 512 + ch:3 * 512 + ch + 1], wa[sl][:, kc, ch * 128:(ch + 1) * 128],
                                     cactT[:, kc:kc + 1], start=(kc == 0), stop=(kc == KC - 1)))
                op("pe", ins, reads=[B_wa[sl], B_const], writes=[BK[3]])
                addc = 1.0 if vi in (1, 3) else 0.0
                op("dve", I("scalar_tensor_tensor", out=shsc[:, vi, half * 4:(half + 1) * 4], in0=pg[:, 3 * 512:3 * 512 + 4],
                            scalar=addc, in1=bT[:, cb * 4:(cb + 1) * 4], op0=ALU.add, op1=ALU.add),
                   reads=[BK[3], B_layer], writes=[B_layer])

        for hp in range(8):
            sl = hp % 2
            op("sp", I("dma_start", out=wo_st[sl][:], in_=wout_d[l, hp * 128:(hp + 1) * 128, :]),
               writes=[B_wo[sl]], dma_sem=s_wo[sl])
            op("dve", I("scalar_tensor_tensor", out=wout2[:, hp, :], in0=wo_st[sl][:], scalar=gT[:, hp:hp + 1],
                        in1=gate_b[:, 0, :], op0=ALU.mult, op1=ALU.mult),
               reads=[B_wo[sl], B_layer], writes=[B_wout2])

        if stop == "L0":
            P.emit()
            return nc
        P.barrier()
        ar.reset()
        win_sb = ar.t("win_sb", [128, KC, 3 * D], BF16)
        B_win = Buf("win_sb")
        xt = [ar.t("xt", [128, 4, D], F32) for _ in range(2)]
        B_xt = [Buf("xt0"), Buf("xt1")]
        W1 = dict(junk=ar.t("junk", [128, D], BF16), B_junk=Buf("junk"),
                  xhat=[ar.t("xhat", [128, D], BF16) for _ in range(2)], B_xhat=[Buf("xh0"), Buf("xh1")],
                  hT=ar.t("hT", [128, KC, 512], BF16), B_hT=Buf("hT"),
                  B_stat=Buf("stat"), eps=ar.t("eps1", [128, 1], F32), B_sc=[Buf("sc%d" % i) for i in range(16)])
        stat = ar.t("stat", [128, 16], F32)
        hT = W1["hT"]; B_hT = W1["B_hT"]
        op("pool", I("memset", W1["eps"][:], EPS), writes=[W1["B_stat"]])
        qk_st = [ar.t("qk_st", [128, 512], BF16) for _ in range(4)]
        B_qk = [Buf("qk%d" % i) for i in range(4)]
        v_st = [ar.t("v_st", [128, 8, DH], BF16) for _ in range(4)]
        B_vst = [Buf("vst%d" % i) for i in range(4)]

        for kc in range(KC if not os.environ.get("NOWIN") else 0):
            op("sp", I("dma_start", out=win_sb[:, kc, :], in_=winb_d[l, kc * 128:(kc + 1) * 128, :]),
               reads=[B_winb[l]], writes=[B_win], dma_sem=s_win)

        if l == 0 and not os.environ.get("NOCAST"):
            cast_ffn(0)
            for l2 in range(1, DEPTH):
                cast_win(l2)
                cast_ffn(l2)

        def load_x(T):
            sl = T % 2
            src = x_src[T * 512:(T + 1) * 512, :].rearrange("(s p) d -> p s d", p=128)
            op("sp", I("dma_start", out=xt[sl][:], in_=src), reads=[B_xsrc], writes=[B_xt[sl]], dma_sem=s_xt[sl])

        hT2 = ar.t("hT_b", [128, KC, 512], BF16)
        hTs = [hT, hT2]
        B_hTs = [B_hT, Buf("hT_b")]

        def norm_a(T, s):
            ci = (T % 2) * 4 + s
            norm_to_hT(xt[T % 2][:, s, :], B_xt[T % 2], s, 0, 1, stat[:, ci:ci + 1], W1, part="a", ci=ci)

        def norm_b(T, s):
            ci = (T % 2) * 4 + s
            norm_to_hT(xt[T % 2][:, s, :], B_xt[T % 2], s, 0, 1, stat[:, ci:ci + 1], W1, part="b",
                       hT=hTs[T % 2], B_hT=B_hTs[T % 2], ci=ci)

        load_x(0)
        if NT > 1:
            load_x(1)
        for s in range(4):
            norm_a(0, s)
            norm_b(0, s)
        stn = 0
        for T in range(NT):
            hT = hTs[T % 2]
            B_hT = B_hTs[T % 2]
            if T + 2 < NT:
                load_x(T + 2)
            gi = [0]

            def after_group():
                g_ = gi[0]
                gi[0] += 1
                if T + 1 < NT:
                    if g_ % 6 == 0:
                        norm_a(T + 1, g_ // 6)
                    if g_ % 6 == 4:
                        norm_b(T + 1, g_ // 6)
            P1PART = int(os.environ.get("P1PART", "3"))
            for cbk in range(24 if P1PART >= 2 else 0):
                typ = cbk // 4
                if typ in (2, 5):
                    continue
                bk = cbk % 2
                grp = 0 if typ < 3 else 1
                isq = typ in (0, 3)
                hp = (cbk % 4) + 4 * grp
                op("pe", [I("matmul", bank(bk), win_sb[:, kc, cbk * 128:(cbk + 1) * 128], hT[:, kc, :],
                            start=(kc == 0), stop=(kc == KC - 1)) for kc in range(KC)],
                   reads=[B_win, B_hT], writes=[BK[bk]])
                sl = stn % 4
                stn += 1
                scl = 0.125 if isq else 1.0
                if stn % 2:
                    op("act", I("activation", qk_st[sl][:], bank(bk), AF.Copy, scale=scl), reads=[BK[bk]], writes=[B_qk[sl]])
                else:
                    op("dve", I("tensor_scalar", out=qk_st[sl][:], in0=bank(bk), scalar1=scl, scalar2=None, op0=ALU.mult),
                       reads=[BK[bk]], writes=[B_qk[sl]])
                dstt = qT_d if isq else kT_d
                B_dst = B_qT if isq else B_kT
                op("sp", I("dma_start", out=dstt[hp * 128:(hp + 1) * 128, T * 512:(T + 1) * 512], in_=qk_st[sl][:]),
                   reads=[B_qk[sl]], writes=[B_dst], dma_sem=s_qk[sl])
                after_group()
            for grp in range(2 if P1PART >= 3 else 0):
                c0 = (2 if grp == 0 else 5) * 512
                for s in range(4):
                    bk = 2 + (s % 2)
                    op("pe", [I("matmul", bank(bk), hT[:, kc, s * 128:(s + 1) * 128], win_sb[:, kc, c0:c0 + 512],
                                start=(kc == 0), stop=(kc == KC - 1)) for kc in range(KC)],
                       reads=[B_win, B_hT], writes=[BK[bk]])
                    sl = (grp * 4 + s) % 4
                    op("act", I("activation", v_st[sl][:].rearrange("p a b -> p (a b)"), bank(bk), AF.Copy),
                       reads=[BK[bk]], writes=[B_vst[sl]])
                    op("sp", I("dma_start", out=v_d[grp * 8:(grp + 1) * 8, :, 4 * T + s, :].rearrange("h p d -> p h d"),
                               in_=v_st[sl][:]),
                       reads=[B_vst[sl]], writes=[B_v], dma_sem=s_vst[sl])
                    after_group()

        if stop == "p1":
            P.emit()
            return nc
        P.barrier()
        ar.reset()
        qTh = [ar.t("qTh", [64, S], BF16) for _ in range(2)]
        kTh = [ar.t("kTh", [64, S], BF16) for _ in range(2)]
        vh = [ar.t("vh", [128, NB, 65], BF16) for _ in range(2)]
        B_qh = [Buf("qh0"), Buf("qh1")]; B_kh = [Buf("kh0"), Buf("kh1")]; B_vh = [Buf("vh0"), Buf("vh1")]
        uT = [ar.t("uT", [64, S], BF16) for _ in range(2)]
        B_uT = [Buf("uT0"), Buf("uT1")]
        tmpb = [ar.t("tmpb", [128, 256], F32) for _ in range(2)]
        B_tmpb = [Buf("tmpb0"), Buf("tmpb1")]
        PT = [ar.t("PT", [128, 5, 128], BF16) for _ in range(3)]
        B_PT = [Buf("PT%d" % i) for i in range(3)]
        rcp = ar.t("rcp", [128, 4], F32)
        B_rcp = Buf("rcp")
        ubf = [ar.t("ubf", [128, 64], BF16) for _ in range(2)]
        B_ubf = [Buf("ubf0"), Buf("ubf1")]
        e_sb = [ar.t("e_sb", [128, 1024], F32) for _ in range(3)]
        B_e = [Buf("e%d" % i) for i in range(3)]
        sp_sb = [ar.t("sp_sb", [128, 1024], BF16) for _ in range(4)]
        B_sp = [Buf("sp%d" % i) for i in range(4)]
        acc = ar.t("acc", [128, 512], BF16)
        B_acc = Buf("acc")
        W_sb = [ar.t("W_sb", [128, 1024], BF16) for _ in range(4)]
        B_W = [Buf("W%d" % i) for i in range(4)]
        one1 = ar.t("one1", [128, 1], F32)
        B_one1 = Buf("one1")
        op("pool", I("memset", one1[:], 1.0), writes=[B_one1])
        op("pool", I("memset", vh[0][:, :, 64:65], 1.0), writes=[B_vh[0]])
        op("pool", I("memset", vh[1][:, :, 64:65], 1.0), writes=[B_vh[1]])

        def load_head(hd):
            sl = hd % 2
            op("sp", I("dma_start", out=qTh[sl][:], in_=qT_d[hd * 64:(hd + 1) * 64, :]),
               reads=[B_qT], writes=[B_qh[sl]], dma_sem=s_hd[sl])
            op("sp", I("dma_start", out=kTh[sl][:], in_=kT_d[hd * 64:(hd + 1) * 64, :]),
               reads=[B_kT], writes=[B_kh[sl]], dma_sem=s_hd[sl])
            op("sp", I("dma_start", out=vh[sl][:, :, 0:64], in_=v_d[hd]),
               reads=[B_v], writes=[B_vh[sl]], dma_sem=s_hd[sl])

        load_head(0)
        NHA = min(8, int(os.environ.get("NHMAX", NH)))
        citems = [(hd, p) for hd in range(NHA) for p in range(NB)]
        NCI = len(citems)

        def hvs(hd):
            hs = hd % 2
            return qTh[hs], kTh[hs], vh[hs], B_qh[hs], B_kh[hs], B_vh[hs], uT[hs], B_uT[hs]

        def jvalid(p):
            return [j for j in range(5) if p - 4 + j >= 0]

        def C0(n):
            hd, p = citems[n]
            q_, k_, v_, Bq, Bk_, Bv, u_, Bu = hvs(hd)
            ps2 = 2 * (n % 2)
            op("pe", [I("matmul", pg[:, ps2 * 512 + j * 128: ps2 * 512 + (j + 1) * 128],
                        k_[:, (p - 4 + j) * 128:(p - 3 + j) * 128], q_[:, p * 128:(p + 1) * 128],
                        start=True, stop=True) for j in jvalid(p)],
               reads=[Bq, Bk_], writes=[BK[ps2], BK[ps2 + 1]])

        def C1(n):
            hd, p = citems[n]
            ps2 = 2 * (n % 2)
            jv = jvalid(p)
            j0 = jv[0]
            pti = n % 3
            nlow = len([j for j in jv if j < 3])
            if nlow > 0:
                op("act", I("activation", PT[pti][:, j0:j0 + nlow, :].rearrange("p a b -> p (a b)"),
                            pg[:, ps2 * 512 + j0 * 128: ps2 * 512 + (j0 + nlow) * 128], AF.Exp,
                            bias=cconst[:, hd:hd + 1]),
                   reads=[BK[ps2], B_layer], writes=[B_PT[pti]])
            jh0 = max(3, j0)
            nh_ = 5 - jh0
            tb = n % 2
            op("dve", I("tensor_tensor", out=tmpb[tb][:, 0:nh_ * 128],
                        in0=pg[:, ps2 * 512 + jh0 * 128: ps2 * 512 + 640],
                        in1=biasT[:, hd, (jh0 - 3) * 128:256], op=ALU.add),
               reads=[BK[ps2], BK[ps2 + 1], B_layer], writes=[B_tmpb[tb]])

        def C2(n):
            hd, p = citems[n]
            jv = jvalid(p)
            j0 = jv[0]
            pti = n % 3
            jh0 = max(3, j0)
            nh_ = 5 - jh0
            tb = n % 2
            op("act", I("activation", PT[pti][:, jh0:5, :].rearrange("p a b -> p (a b)"),
                        tmpb[tb][:, 0:nh_ * 128], AF.Exp),
               reads=[B_tmpb[tb]], writes=[B_PT[pti]])
            if j0 == 0:
                op("pool", I("memset", PT[pti][0:64, 0, 64:128], 0.0), reads=[B_PT[pti]], writes=[B_PT[pti]])

        def C3(n):
            hd, p = citems[n]
            q_, k_, v_, Bq, Bk_, Bv, u_, Bu = hvs(hd)
            jv = jvalid(p)
            pti = n % 3
            ob = 4 + (n % 2)
            op("pe", [I("matmul", bank(ob, 65), PT[pti][:, j, :], v_[:, p - 4 + j, :],
                        start=(j == jv[0]), stop=(j == jv[-1])) for j in jv],
               reads=[B_PT[pti], Bv], writes=[BK[ob]])

        def C4(n):
            ob = 4 + (n % 2)
            rc = rcp[:, (n % 4):(n % 4) + 1]
            op("dve", I("reciprocal", rc, pg[:, ob * 512 + 64: ob * 512 + 65]), reads=[BK[ob]], writes=[B_rcp])
            op("dve", I("tensor_scalar", out=ubf[n % 2][:], in0=bank(ob, 64), scalar1=rc, scalar2=None, op0=ALU.mult),
               reads=[BK[ob], B_rcp], writes=[B_ubf[n % 2]])

        def C5(n):
            us = n % 2
            op("pe", I("transpose", out=ptb[us][0:64, 0:128], in_=ubf[us][:], identity=ident_bf[:]),
               reads=[B_ubf[us], B_const], writes=[B_pt[us]])

        def C6(n):
            hd, p = citems[n]
            q_, k_, v_, Bq, Bk_, Bv, u_, Bu = hvs(hd)
            us = n % 2
            op("act", I("activation", u_[:, p * 128:(p + 1) * 128], ptb[us][0:64, 0:128], AF.Copy),
               reads=[B_pt[us]], writes=[Bu])
            if p == NB - 1:
                op("sp", I("dma_start", out=aT_d[hd], in_=u_[:]), reads=[Bu], writes=[B_aT], dma_sem=s_uT[hd % 2])
            if p == 0 and hd + 1 < NH:
                load_head(hd + 1)

        for t in range(NCI + 6):
            if t < NCI:
                C0(t)
            if 0 <= t - 3 < NCI:
                C3(t - 3)
            if 0 <= t - 5 < NCI:
                C5(t - 5)
            if 0 <= t - 1 < NCI:
                C1(t - 1)
            if 0 <= t - 2 < NCI:
                C2(t - 2)
            if 0 <= t - 4 < NCI:
                C4(t - 4)
            if 0 <= t - 6 < NCI:
                C6(t - 6)

        acc3 = [acc] + [ar.t("acc_x", [128, 512], BF16) for _ in range(2)]
        B_acc3 = [B_acc, Buf("acc1"), Buf("acc2")]
        dmaskR = ar.t("dmaskR", [128, 4, 512], BF16)
        B_dmR = Buf("dmaskR")
        op("pool", [I("tensor_copy", dmaskR[:, r, :], dmask[:, 3 - r, :]) for r in range(4)], reads=[B_const], writes=[B_dmR])
        pso = [ptb[0].bitcast(F32), ptb[1].bitcast(F32)]
        items = []
        for hd in range(8, int(os.environ.get("NHMAX", NH))):
            for T in range(NT):
                npair = 2 * T + 2
                for i in range(npair):
                    items.append((hd, T, i, npair))
        NI = len(items)

        def hv(hd):
            hs = hd % 2
            return qTh[hs], kTh[hs], vh[hs], B_qh[hs], B_kh[hs], B_vh[hs], uT[hs], B_uT[hs]

        def zpair(n):
            b3 = n % 3
            return pg[:, b3 * 1024:(b3 + 1) * 1024], [BK[2 * b3], BK[2 * b3 + 1]]

        def kbs_of(n):
            hd, T, i, npair = items[n]
            khi = 2 * npair - 1 - 2 * i
            return khi, khi - 1

        def S0(n):
            hd, T, i, npair = items[n]
            q_, k_, v_, Bq, Bk_, Bv, u_, Bu = hv(hd)
            zp, Bz = zpair(n)
            khi, klo = kbs_of(n)
            qs = q_[:, T * 512:(T + 1) * 512]
            op("pe", [I("matmul", zp[:, 0:512], k_[:, khi * 128:(khi + 1) * 128], qs, start=True, stop=True),
                      I("matmul", zp[:, 512:1024], k_[:, klo * 128:(klo + 1) * 128], qs, start=True, stop=True)],
               reads=[Bq, Bk_], writes=Bz)

        def S1(n):
            zp, Bz = zpair(n)
            op("act", I("activation", e_sb[n % 3][:], zp, AF.Exp), reads=Bz, writes=[B_e[n % 3]])

        def S2(n):
            hd, T, i, npair = items[n]
            khi, klo = kbs_of(n)
            sp_ = sp_sb[n % 4]
            op("act", I("activation", sp_[:], e_sb[n % 3][:], AF.Ln, bias=one1[:, 0:1]),
               reads=[B_e[n % 3], B_one1], writes=[B_sp[n % 4]])
            if khi >= 4 * T:
                r0 = 3 - (khi - 4 * T)
                op("dve", I("tensor_tensor", out=sp_[:], in0=sp_[:], in1=dmaskR[:, r0:r0 + 2, :].rearrange("p a b -> p (a b)"),
                            op=ALU.mult),
                   reads=[B_sp[n % 4], B_dmR], writes=[B_sp[n % 4]])
            if i + 1 < npair:
                if i == 0:
                    op("pool", I("tensor_tensor", out=acc3[(n + 1) % 3][:], in0=sp_[:, 0:512], in1=sp_[:, 512:1024], op=ALU.add),
                       reads=[B_sp[n % 4]], writes=[B_acc3[(n + 1) % 3]])
                else:
                    op("pool", I("tensor_tensor", out=acc3[(n + 1) % 3][:], in0=acc3[n % 3][:], in1=sp_[:, 0:512], op=ALU.add),
                       reads=[B_acc3[n % 3], B_sp[n % 4]], writes=[B_acc3[(n + 1) % 3]])
                    op("pool", I("tensor_tensor", out=acc3[(n + 1) % 3][:], in0=acc3[(n + 1) % 3][:], in1=sp_[:, 512:1024], op=ALU.add),
                       reads=[B_acc3[(n + 1) % 3], B_sp[n % 4]], writes=[B_acc3[(n + 1) % 3]])

        def S3(n):
            hd, T, i, npair = items[n]
            zp, Bz = zpair(n)
            sp_ = sp_sb[n % 4]
            hi, lo = zp[:, 0:512], zp[:, 512:1024]
            ins = [I("matmul", hi, triN[:], sp_[:, 0:512], start=False, stop=(i == 0), skip_group_check=True)]
            if i > 0:
                ins.append(I("matmul", hi, onesN[:], acc3[n % 3][:], start=False, stop=True, skip_group_check=True))
            ins += [I("matmul", lo, triN[:], sp_[:, 512:1024], start=False, stop=False, skip_group_check=True),
                    I("matmul", lo, onesN[:], sp_[:, 0:512], start=False, stop=(i == 0), skip_group_check=True)]
            if i > 0:
                ins.append(I("matmul", lo, onesN[:], acc3[n % 3][:], start=False, stop=True, skip_group_check=True))
            rd = [B_sp[n % 4], B_const] + ([B_acc3[n % 3]] if i > 0 else [])
            op("pe", ins, reads=rd, writes=Bz)

        def S4(n):
            hd, T, i, npair = items[n]
            khi, klo = kbs_of(n)
            w_ = W_sb[n % 4]
            zp, Bz = zpair(n)
            op("act", I("activation", w_[:], zp, AF.Exp), reads=Bz, writes=[B_W[n % 4]])
            if khi >= 4 * T:
                r0 = 3 - (khi - 4 * T)
                op("dve", I("tensor_tensor", out=w_[:], in0=w_[:], in1=dmaskR[:, r0:r0 + 2, :].rearrange("p a b -> p (a b)"),
                            op=ALU.mult),
                   reads=[B_W[n % 4], B_dmR], writes=[B_W[n % 4]])

        def S5(n):
            hd, T, i, npair = items[n]
            q_, k_, v_, Bq, Bk_, Bv, u_, Bu = hv(hd)
            khi, klo = kbs_of(n)
            g = (hd - 8) * NT + T
            po = pso[g % 2]
            w_ = W_sb[n % 4]
            op("pe", [I("matmul", po[0:64, 0:512], v_[:, khi, 0:64], w_[:, 0:512], start=(i == 0), stop=False),
                      I("matmul", po[0:64, 0:512], v_[:, klo, 0:64], w_[:, 512:1024], start=False, stop=(i == npair - 1))],
               reads=[B_W[n % 4], Bv], writes=[B_pt[g % 2]])
            if i == npair - 1:
                op("dve", I("tensor_copy", u_[:, T * 512:(T + 1) * 512], po[0:64, 0:512]),
                   reads=[B_pt[g % 2]], writes=[Bu])
                if T == NT - 1:
                    op("sp", I("dma_start", out=aT_d[hd], in_=u_[:]), reads=[Bu], writes=[B_aT], dma_sem=s_uT[hd % 2])

        for t in range(NI + 4):
            if t < NI:
                S0(t)
            if 0 <= t - 1 < NI:
                S2(t - 1)
            if t < NI:
                S1(t)
            if 0 <= t - 2 < NI:
                S3(t - 2)
                S4(t - 2)
            if 0 <= t - 3 < NI:
                S5(t - 3)
                hd5, T5, i5, _ = items[t - 3]
                if T5 == 0 and i5 == 0 and hd5 + 1 < NH:
                    load_head(hd5 + 1)

        if stop == "p2":
            P.emit()
            return nc
        P.barrier()
        ar.reset()
        x3s = [ar.t("x3", [128, 4, D], F32) for _ in range(2)]
        B_x3s = [Buf("x3a"), Buf("x3b")]
        aTt = ar.t("aTt", [128, 8, 512], BF16)
        B_aTt = Buf("aTt")
        W3 = dict(junk=ar.t("junk3", [128, D], BF16), B_junk=Buf("junk3"),
                  xhat=[ar.t("xhat3", [128, D], BF16) for _ in range(2)], B_xhat=[Buf("xh3_0"), Buf("xh3_1")],
                  hT=ar.t("hT3", [128, KC, 512], BF16), B_hT=Buf("hT3"),
                  B_stat=Buf("stat3"), eps=ar.t("eps3", [128, 1], F32), B_sc=[Buf("sc3_%d" % i) for i in range(16)])
        stat = ar.t("stat3", [128, 16], F32)
        hT = W3["hT"]; B_hT = W3["B_hT"]; B_stat = W3["B_stat"]; junk = W3["junk"]; B_junk = W3["B_junk"]
        eps3 = W3["eps"]
        op("pool", I("memset", eps3[:], EPS), writes=[B_stat])
        sq = [ar.t("sq", [128, 512], F32) for _ in range(2)]
        B_sq = [Buf("sq0"), Buf("sq1")]
        rstd2 = ar.t("rstd2", [128, 4, 2], F32)
        B_rstd2 = Buf("rstd2")
        wup_sb = [ar.t("wup_sb", [128, KC, 2, 128], BF16) for _ in range(3)]
        B_wup = [Buf("wup%d" % i) for i in range(3)]
        wdn_sb = [ar.t("wdn_sb", [128, 4, 512], BF16) for _ in range(3)]
        B_wdn = [Buf("wdn%d" % i) for i in range(3)]
        aF = ar.t("aF", [128, NJ, 512], BF16)
        B_aF = Buf("aF")
        y0 = [ar.t("y0", [128, 512], F32) for _ in range(6)]
        B_y0 = [Buf("y0_%d" % i) for i in range(6)]
        addv = ar.t("addv", [128, 44, 2], F32)
        tAB = ar.t("tAB", [128, 2, 44], F32)
        B_addv = Buf("addv")
        sg = [ar.t("sg", [128, 512], F32) for _ in range(2)]
        B_sg = [Buf("sg0"), Buf("sg1")]
        tmp3 = [ar.t("tmp3", [128, 512], F32) for _ in range(2)]
        B_tmp3 = [Buf("tmp3_0"), Buf("tmp3_1")]

        wupc = 0
        wdnc = 0
        def load_x3(T):
            src = x_src[T * 512:(T + 1) * 512, :].rearrange("(s p) d -> p s d", p=128)
            op("sp", I("dma_start", out=x3s[T % 2][:], in_=src), reads=[B_xsrc], writes=[B_x3s[T % 2]], dma_sem=s_x3l[T % 2])

        def load_aTt(T):
            for two in range(2):
                srca = aT_d[:, :, T * 512:(T + 1) * 512].rearrange("(hp two) d t -> two d hp t", two=2)[two]
                op("sp", I("dma_start", out=aTt[two * 64:(two + 1) * 64, :, :], in_=srca),
                   reads=[B_aT], writes=[B_aTt], dma_sem=s_aTt)

        load_x3(0)
        load_aTt(0)
        for T in range(NT):
            x3 = x3s[T % 2]
            B_x3 = B_x3s[T % 2]
            if T + 1 < NT:
                load_x3(T + 1)
            for g in range(2):
                for hh in range(4):
                    hp = g * 4 + hh
                    sl = hp % 2
                    op("act", I("activation", sq[sl][:], aTt[:, hp, :], AF.Square), reads=[B_aTt], writes=[B_sq[sl]])
                    op("pe", [I("matmul", pg[:, s * 512 + g: s * 512 + g + 1], sq[sl][:, s * 128:(s + 1) * 128],
                                ones_col[:, :], start=(hh == 0), stop=(hh == 3)) for s in range(4)],
                       reads=[B_sq[sl], B_const], writes=[BK[0], BK[1], BK[2], BK[3]])
            op("act", [I("activation", rstd2[:, s, :], pg[:, s * 512: s * 512 + 2], AF.Ln, bias=eps3[:, 0:1], scale=1.0 / 512)
                       for s in range(4)],
               reads=[BK[0], BK[1], BK[2], BK[3], B_stat], writes=[B_rstd2])
            op("act", I("activation", rstd2[:].rearrange("p a b -> p (a b)"), rstd2[:].rearrange("p a b -> p (a b)"),
                        AF.Exp, scale=-0.5), reads=[B_rstd2], writes=[B_rstd2])
            for s in range(4):
                for n in range(2):
                    bA = 2 * ((s * 2 + n) % 2)
                    bB = bA + 1
                    ins = [I("matmul", bank(bA), aTt[:, hp, s * 128:(s + 1) * 128], wout2[:, hp, n * 512:(n + 1) * 512],
                             start=(hp == 0), stop=(hp == 3)) for hp in range(4)]
                    ins += [I("matmul", bank(bB), aTt[:, hp, s * 128:(s + 1) * 128], wout2[:, hp, n * 512:(n + 1) * 512],
                              start=(hp == 4), stop=(hp == 7)) for hp in range(4, 8)]
                    op("pe", ins, reads=[B_aTt, B_wout2], writes=[BK[bA], BK[bB]])
                    xv = x3[:, s, n * 512:(n + 1) * 512]
                    op("dve", I("scalar_tensor_tensor", out=xv, in0=bank(bA), scalar=rstd2[:, s, 0:1], in1=xv,
                                op0=ALU.mult, op1=ALU.add), reads=[BK[bA], B_rstd2, B_x3], writes=[B_x3])
                    op("dve", I("scalar_tensor_tensor", out=xv, in0=bank(bB), scalar=rstd2[:, s, 1:2], in1=xv,
                                op0=ALU.mult, op1=ALU.add), reads=[BK[bB], B_rstd2, B_x3], writes=[B_x3])
            if T + 1 < NT:
                load_aTt(T + 1)
            for s in range(4):
                norm_to_hT(x3[:, s, :], B_x3, s, 2, 3, stat[:, s:s + 1], W3, ci=s)
            cold = carry[:, (T + 1) % 2, :, :]
            cnew = carry[:, T % 2, :, :]
            Bcold = B_carry[(T + 1) % 2]
            Bcnew = B_carry[T % 2]
            op("dve", [I("tensor_tensor", out=tAB[:, 0, :], in0=cold[:, :, 0], in1=cwT[:, 0:44], op=ALU.mult),
                       I("tensor_tensor", out=tAB[:, 1, :], in0=cold[:, :, 1], in1=cwT[:, 44:88], op=ALU.mult),
                       I("tensor_tensor", out=addv[:, :, 1], in0=cold[:, :, 1], in1=cwT[:, 0:44], op=ALU.mult)],
               reads=[Bcold, B_layer], writes=[B_addv])
            op("dve", I("tensor_tensor", out=addv[:, :, 0], in0=tAB[:, 0, :], in1=tAB[:, 1, :], op=ALU.add),
               reads=[B_addv], writes=[B_addv])
            def ffn_fin(jj):
                u0 = (2 * jj) % 6
                u1 = (2 * jj + 1) % 6
                op("act", I("activation", sg[jj % 2][:], y0[u0][:], AF.Silu), reads=[B_y0[u0]], writes=[B_sg[jj % 2]])
                op("pool", I("tensor_tensor", out=aF[:, jj, :], in0=y0[u1][:], in1=sg[jj % 2][:], op=ALU.mult),
                   reads=[B_y0[u1], B_sg[jj % 2]], writes=[B_aF])

            for j in range(NJ):
                wsl = wupc % 3
                wupc += 1
                if not os.environ.get("NOWLOAD"):
                    op("sp", I("dma_start", out=wup_sb[wsl][:], in_=wupb_d[l, j]),
                       reads=[B_wupb[l]], writes=[B_wup[wsl]], dma_sem=s_wup[wsl])
                for g in range(2):
                    bk = 2 * (j % 3) + g
                    ui = (2 * j + g) % 6
                    jg = g * NJ + j
                    op("pe", [I("matmul", bank(bk), wup_sb[wsl][:, kc, g, :], hT[:, kc, :], start=(kc == 0), stop=(kc == KC - 1))
                              for kc in range(KC)],
                       reads=[B_wup[wsl], B_hT], writes=[BK[bk]])
                    filler()
                    op("act", I("activation", y0[ui][:], bank(bk), AF.Identity, bias=cbT[:, jg:jg + 1],
                                scale=cwT[:, 88 + jg:88 + jg + 1]),
                       reads=[BK[bk], B_layer], writes=[B_y0[ui]])
                    op("act", I("activation", cnew[:, jg, :], pg[:, bk * 512 + 510: bk * 512 + 512], AF.Copy),
                       reads=[BK[bk]], writes=[Bcnew])
                    op("dve", I("scalar_tensor_tensor", out=y0[ui][:, 1:512], in0=pg[:, bk * 512: bk * 512 + 511],
                                scalar=cwT[:, 44 + jg:44 + jg + 1], in1=y0[ui][:, 1:512], op0=ALU.mult, op1=ALU.add),
                       reads=[BK[bk], B_y0[ui], B_layer], writes=[B_y0[ui]])
                    op("dve", I("scalar_tensor_tensor", out=y0[ui][:, 2:512], in0=pg[:, bk * 512: bk * 512 + 510],
                                scalar=cwT[:, jg:jg + 1], in1=y0[ui][:, 2:512], op0=ALU.mult, op1=ALU.add),
                       reads=[BK[bk], B_y0[ui], B_layer], writes=[B_y0[ui]])
                    op("pool", I("tensor_tensor", out=y0[ui][:, 0:2], in0=y0[ui][:, 0:2], in1=addv[:, jg, :], op=ALU.add),
                       reads=[B_addv, B_y0[ui]], writes=[B_y0[ui]])
                    if g == 1 and j >= 1:
                        ffn_fin(j - 1)
            ffn_fin(NJ - 1)
            for n in range(2):
                for jg0 in range(0, NJ, 4):
                    nj = min(4, NJ - jg0)
                    dsl = wdnc % 3
                    wdnc += 1
                    src = wdnb_d[l, jg0 * 128:(jg0 + nj) * 128, n * 512:(n + 1) * 512].rearrange("(j p) c -> p j c", p=128)
                    if not os.environ.get("NOWLOAD"):
                        op("sp", I("dma_start", out=wdn_sb[dsl][:, 0:nj, :], in_=src),
                           reads=[B_wdnb[l]], writes=[B_wdn[dsl]], dma_sem=s_wdn[dsl])
                    ins = []
                    for s in range(4):
                        for jj in range(nj):
                            j = jg0 + jj
                            ins.append(I("matmul", bank(2 + s), aF[:, j, s * 128:(s + 1) * 128], wdn_sb[dsl][:, jj, :],
                                         start=(j == 0), stop=(j == NJ - 1)))
                    op("pe", ins, reads=[B_aF, B_wdn[dsl]], writes=[BK[2], BK[3], BK[4], BK[5]])
                    filler()
                for s in range(4):
                    ts = s % 2
                    xv = x3[:, s, n * 512:(n + 1) * 512]
                    op("dve", I("tensor_tensor", out=tmp3[ts][:], in0=bank(2 + s), in1=gate_b[:, 1, n * 512:(n + 1) * 512],
                                op=ALU.mult),
                       reads=[BK[2 + s], B_layer], writes=[B_tmp3[ts]])
                    op("pool", I("tensor_tensor", out=xv, in0=xv, in1=tmp3[ts][:], op=ALU.add),
                       reads=[B_tmp3[ts], B_x3], writes=[B_x3])
            if last:
                for s in range(4):
                    sc = stat[:, 8 + s:9 + s]
                    op("act", I("activation", junk[:], x3[:, s, :], AF.Square, accum_out=sc),
                       reads=[B_x3], writes=[B_junk, B_stat])
                    op("act", I("activation", sc, sc, AF.Ln, bias=eps3[:, 0:1], scale=1.0 / D), reads=[B_stat], writes=[B_stat])
                    op("act", I("activation", sc, sc, AF.Exp, scale=-0.5), reads=[B_stat], writes=[B_stat])
                    op("dve", I("scalar_tensor_tensor", out=x3[:, s, :], in0=x3[:, s, :], scalar=sc, in1=fg_b[:],
                                op0=ALU.mult, op1=ALU.mult),
                       reads=[B_x3, B_stat, B_const], writes=[B_x3])
            dst = x_dst[T * 512:(T + 1) * 512, :].rearrange("(s p) d -> p s d", p=128)
            op("sp", I("dma_start", out=dst, in_=x3[:]), reads=[B_x3], writes=[B_xdst], dma_sem=s_x3o)

    P.emit()
    return nc


_CACHE = {}


def make_in_maps(inputs, S, DEPTH, ncores):
    f = lambda a: np.ascontiguousarray(np.asarray(a, dtype=np.float32))
    x = f(inputs["x"]); c = f(inputs["c"])
    g = np.concatenate([f(inputs["g_a"]).reshape(DEPTH, 8, 64), f(inputs["g_b"]).reshape(DEPTH, 8, 64)], axis=1)
    shared = dict(
        w_ada=f(inputs["w_ada"]), b_ada=f(inputs["b_ada"]), w_in=f(inputs["w_in"]),
        rel_bias=f(inputs["rel_bias"]), g=np.ascontiguousarray(g), w_out=f(inputs["w_out"]),
        w_up=f(inputs["w_up"]), conv_w=f(inputs["conv_w"]).reshape(DEPTH, 132, 128),
        conv_b=f(inputs["conv_b"]).reshape(DEPTH, 44, 128), w_down=f(inputs["w_down"]),
        final_g=f(inputs["final_g"]).reshape(1, D),
    )
    maps = []
    for b in range(ncores):
        m = dict(shared)
        m["x"] = np.ascontiguousarray(x[b])
        m["c"] = np.ascontiguousarray(c[b].reshape(8, 128))
        maps.append(m)
    return maps


def kernel(x, c, w_ada, b_ada, w_in, rel_bias, g_a, g_b, w_out, w_up, conv_w, conv_b, w_down, final_g):
    inputs = dict(x=x, c=c, w_ada=w_ada, b_ada=b_ada, w_in=w_in, rel_bias=rel_bias, g_a=g_a, g_b=g_b,
                  w_out=w_out, w_up=w_up, conv_w=conv_w, conv_b=conv_b, w_down=w_down, final_g=final_g)
    Bn, S, _ = np.asarray(x).shape
    DEPTH = np.asarray(w_in).shape[0]
    key = (S, DEPTH)
    if key not in _CACHE:
        _CACHE[key] = build(S, DEPTH)
    nc = _CACHE[key]
    maps = make_in_maps(inputs, S, DEPTH, Bn)
    res = run_bass_kernel_spmd(nc, maps, core_ids=list(range(Bn)))
    out = np.stack([np.asarray(r["out"]) for r in res.results], axis=0)
    return out.astype(np.float32)
```

```python
import numpy as np
import concourse.bass as bass
import concourse.mybir as mybir
from concourse.bass_utils import run_bass_kernel_spmd

F32 = mybir.dt.float32
BF16 = mybir.dt.bfloat16
AF = mybir.ActivationFunctionType
ALU = mybir.AluOpType

D = 1024
KC = 8
NH = 16
DH = 64
DFF = 2816
NJ = 22
EPS = 1e-6
NEG = -30000.0


import os as _os
STRICT = _os.environ.get("KSTRICT", "1") != "0"


class Buf:
    __slots__ = ("name", "writers", "readers", "excl")

    def __init__(self, name, excl=False):
        self.name = name
        self.writers = {}
        self.readers = {}
        self.excl = excl


class SemObj:
    __slots__ = ("h", "count", "name", "is_dma")

    def __init__(self, h, name):
        self.h = h
        self.count = 0
        self.name = name
        self.is_dma = False


class Eng:
    def __init__(self, name, handle, sem):
        self.name = name
        self.handle = handle
        self.sem = sem
        self.ops = []
        self.seen = {}


class Prog:
    def __init__(self, nc):
        self.nc = nc
        self.engs = {}
        self.sems = []

    def new_sem(self, name):
        s = SemObj(self.nc.alloc_semaphore(name=name), name)
        self.sems.append(s)
        return s

    def add_engine(self, name, handle):
        e = Eng(name, handle, self.new_sem("e_" + name))
        self.engs[name] = e
        return e

    def op(self, eng, fn, reads=(), writes=(), dma_sem=None, extra=()):
        if isinstance(fn, tuple):
            fn = [fn]
        ex_r = [b for b in reads if b.excl]
        if ex_r:
            reads = [b for b in reads if not b.excl]
            writes = list(writes) + [b for b in ex_r if b not in writes]
        e = self.engs[eng]
        is_dma = dma_sem is not None
        if not is_dma and e.sem.count >= 60000:
            e.sem = self.new_sem("e_%s_%d" % (eng, len(self.sems)))
        waits = {}

        def addw(s, v):
            if waits.get(s, -1) < v:
                waits[s] = v

        for b in reads:
            for s, v in b.writers.items():
                addw(s, v)
        for b in writes:
            skip_same = (not STRICT) or (eng == "pe" and b.excl)
            for s, v in b.writers.items():
                if s is e.sem and not is_dma and skip_same:
                    continue
                addw(s, v)
            for s, v in b.readers.items():
                if s is e.sem and not is_dma and skip_same:
                    continue
                addw(s, v)
        for s, v in extra:
            addw(s, v)
        wl = []
        for s, v in list(waits.items()):
            if s.is_dma:
                v = s.count
            if e.seen.get(s, -1) >= v:
                continue
            e.seen[s] = v
            wl.append((s, v))
        if is_dma:
            dma_sem.is_dma = True
            dma_sem.count += 16
            tok = (dma_sem, dma_sem.count)
            e.ops.append((wl, fn, dma_sem, 16))
        else:
            e.sem.count += 1
            tok = (e.sem, e.sem.count)
            e.ops.append((wl, fn, e.sem, 1))
        for b in reads:
            if b.readers.get(tok[0], -1) < tok[1]:
                b.readers[tok[0]] = tok[1]
        for b in writes:
            if b.writers.get(tok[0], -1) < tok[1]:
                b.writers[tok[0]] = tok[1]
        return tok

    def barrier(self):
        for e in self.engs.values():
            wl = []
            for s in self.sems:
                if s.count > 0 and e.seen.get(s, -1) < s.count:
                    e.seen[s] = s.count
                    wl.append((s, s.count))
            if wl:
                e.ops.append((wl, None, None, 0))

    def emit(self):
        nc = self.nc
        self.barrier()
        with nc.Block() as block:
            def mk(e):
                def body(h):
                    for wl, fn, isem, ival in e.ops:
                        for s, v in wl:
                            h.wait_ge(s.h, v)
                        if fn is not None:
                            ins = None
                            for (m, a, k) in fn:
                                ins = getattr(h, m)(*a, **k)
                            ins.then_inc(isem.h, ival)
                return body
            decs = {"pe": block.tensor, "act": block.scalar, "dve": block.vector,
                    "pool": block.gpsimd, "sp": block.sync}
            for name, e in self.engs.items():
                decs[name](mk(e))


def I(m, *a, **k):
    return (m, a, k)


class Arena:
    def __init__(self, nc, base, limit):
        self.nc = nc
        self.base = base
        self.top = base
        self.limit = limit
        self.n = 0

    def t(self, name, shape, dtype):
        esz = 4 if dtype == F32 else 2
        n = esz
        for s in shape[1:]:
            n *= s
        off = (self.top + 63) // 64 * 64
        assert off + n <= self.limit, ("SBUF overflow", name, off, n, self.limit)
        self.top = off + n
        self.n += 1
        return self.nc.alloc_sbuf_tensor_at("%s_%d" % (name, self.n), list(shape), dtype, offset=off)

    def reset(self):
        self.top = self.base


def build(S, DEPTH, dbg=False, stop=None):
    import os
    nc = bass.Bass("TRN2", target_bir_lowering=False)
    NT = S // 512
    NB = S // 128
    skind = "ExternalOutput" if dbg else "Internal"

    def din(name, shape, dt=F32):
        return nc.dram_tensor(name, list(shape), dt, kind="ExternalInput")

    x_d = din("x", [S, D])
    c_d = din("c", [8, 128])
    wada_d = din("w_ada", [DEPTH, D, 6 * D])
    bada_d = din("b_ada", [DEPTH, 6 * D])
    win_d = din("w_in", [DEPTH, D, 3 * D])
    relb_d = din("rel_bias", [DEPTH, 8, 257])
    g_d = din("g", [DEPTH, 16, 64])
    wout_d = din("w_out", [DEPTH, D, D])
    wup_d = din("w_up", [DEPTH, D, 2 * DFF])
    convw_d = din("conv_w", [DEPTH, 132, 128])
    convb_d = din("conv_b", [DEPTH, 44, 128])
    wdn_d = din("w_down", [DEPTH, DFF, D])
    fg_d = din("final_g", [1, D])
    out_d = nc.dram_tensor("out", [S, D], F32, kind="ExternalOutput")

    def dscr(name, shape, dt):
        return nc.dram_tensor(name, list(shape), dt, kind=skind)

    winb_d = nc.dram_tensor("win_b", [DEPTH, D, 3 * D], BF16, kind="Internal")
    wupb_d = nc.dram_tensor("wup_b", [DEPTH, NJ, 128, KC, 2, 128], BF16, kind="Internal")
    wdnb_d = nc.dram_tensor("wdn_b", [DEPTH, DFF, D], BF16, kind="Internal")
    ext_d = nc.dram_tensor("ext_s", [8, 640], F32, kind="Internal")
    qT_d = dscr("qT_s", [NH * DH, S], BF16)
    kT_d = dscr("kT_s", [NH * DH, S], BF16)
    v_d = dscr("v_s", [NH, 128, NB, DH], BF16)
    aT_d = dscr("aT_s", [NH, DH, S], BF16)
    xs_d = dscr("xs_s", [S, D], F32)

    P = Prog(nc)
    for n, h in (("pe", nc.tensor), ("act", nc.scalar), ("dve", nc.vector),
                 ("pool", nc.gpsimd), ("sp", nc.sync)):
        P.add_engine(n, h)
    op = P.op

    SB0 = 16640
    PERS = SB0 + 67584
    pa = Arena(nc, SB0, PERS)
    ident_bf = pa.t("ident_bf", [128, 128], BF16)
    ident_f = pa.t("ident_f", [128, 128], F32)
    Jf = pa.t("Jf", [128, 128], F32)
    triN = pa.t("triN", [128, 128], BF16)
    onesN = pa.t("onesN", [128, 128], BF16)
    dmask = pa.t("dmask", [128, 4, 512], BF16)
    ones_col = pa.t("ones_col", [128, 1], F32)
    ones_row = pa.t("ones_row", [1, 128], F32)
    cactT = pa.t("cactT", [128, 8], F32)
    cact_bc = pa.t("cact_bc", [128, 8, 128], F32)
    shsc = pa.t("shsc", [128, 4, 8], F32)
    gate_b = pa.t("gate_b", [128, 2, 1024], F32)
    bT = pa.t("bT", [128, 48], F32)
    gT = pa.t("gT", [128, 8], F32)
    cwT = pa.t("cwT", [128, 132], F32)
    cbT = pa.t("cbT", [128, 44], F32)
    biasT = pa.t("biasT", [128, 8, 256], F32)
    cconst = pa.t("cconst", [128, 8], F32)
    carry = pa.t("carry", [128, 2, 44, 2], F32)
    fg_b = pa.t("fg_b", [128, 1024], F32)
    wout2 = pa.t("wout2", [128, 8, 1024], BF16)
    B_const = Buf("const")
    B_layer = Buf("layer")
    B_wout2 = Buf("wout2")
    B_carry = [Buf("carry0"), Buf("carry1")]

    ar = Arena(nc, PERS, 229000)

    pg = nc.alloc_psum_tensor("pg", [128, 6 * 512], F32)
    ptb = [nc.alloc_psum_tensor("ptb0", [128, 1024], BF16), nc.alloc_psum_tensor("ptb1", [128, 1024], BF16)]
    BK = [Buf("bank%d" % i, excl=True) for i in range(6)]
    B_pt = [Buf("pt0", excl=True), Buf("pt1", excl=True)]

    def bank(i, n=512):
        return pg[:, i * 512:i * 512 + n]

    NFILL = int(os.environ.get("NFILL", "0"))
    fill_out = ptb[1].bitcast(F32)

    def filler(n=None):
        n = NFILL if n is None else n
        if n <= 0:
            return
        op("pe", [I("matmul", fill_out[:, 0:512], ident_bf[:], dmask[:, 0, :], start=True, stop=True) for _ in range(n)],
           reads=[B_const], writes=[B_pt[1]])

    B_x = Buf("x_in")
    B_winb = [Buf("winb%d" % l) for l in range(DEPTH)]
    B_wupb = [Buf("wupb%d" % l) for l in range(DEPTH)]
    B_wdnb = [Buf("wdnb%d" % l) for l in range(DEPTH)]
    B_ext = Buf("ext")
    B_qT = Buf("qT"); B_kT = Buf("kT"); B_v = Buf("v"); B_aT = Buf("aT"); B_xs = Buf("xs"); B_out = Buf("out")

    cast_sems = [P.new_sem("cast%d" % i) for i in range(4)]
    cast_n = [0]

    def cast_op(dst, src, B, sem_unused=None):
        sem = cast_sems[cast_n[0] % 4]
        cast_n[0] += 1
        ex = [(sem, sem.count)] if sem.count > 0 else []
        tok = op("pool", I("dma_start", out=dst, in_=src), dma_sem=sem, extra=ex)
        B.writers[tok[0]] = tok[1]

    def cast_win(l):
        for kc in range(KC):
            src = win_d[l, kc * 128:(kc + 1) * 128, :].rearrange("p (a b) -> p a b", b=1024)
            dst = winb_d[l, kc * 128:(kc + 1) * 128, :].rearrange("p (a b) -> p a b", b=1024)
            cast_op(dst, src, B_winb[l])

    def cast_ffn(l):
        for j in range(NJ if not os.environ.get("NOWUP") else 0):
            for g in range(2):
                src = wup_d[l, :, g * DFF + j * 128: g * DFF + (j + 1) * 128].rearrange("(kc p) c -> p kc c", p=128)
                dst = wupb_d[l, j, :, :, g, :]
                cast_op(dst, src, B_wupb[l])
        for j in range(NJ if not os.environ.get("NOWDN") else 0):
            src = wdn_d[l, j * 128:(j + 1) * 128, :]
            dst = wdnb_d[l, j * 128:(j + 1) * 128, :]
            cast_op(dst, src, B_wdnb[l])

    op("pool", [I("memset", ident_bf[:], 0.0), I("memset", ident_f[:], 0.0), I("memset", Jf[:], 0.0),
                I("memset", triN[:], -1.0), I("memset", onesN[:], -1.0), I("memset", ones_col[:], 1.0),
                I("memset", ones_row[:], 1.0), I("memset", dmask[:], 1.0)], writes=[B_const])
    ins = [
        I("affine_select", out=ident_bf[:], in_=ident_bf[:], pattern=[[-1, 128]], compare_op=ALU.not_equal,
          fill=1.0, base=0, channel_multiplier=1),
        I("affine_select", out=ident_f[:], in_=ident_f[:], pattern=[[-1, 128]], compare_op=ALU.not_equal,
          fill=1.0, base=0, channel_multiplier=1),
        I("affine_select", out=Jf[:], in_=Jf[:], pattern=[[1, 128]], compare_op=ALU.not_equal,
          fill=1.0, base=-127, channel_multiplier=1),
        I("affine_select", out=triN[:], in_=triN[:], pattern=[[-1, 128]], compare_op=ALU.is_ge,
          fill=0.0, base=0, channel_multiplier=1),
    ]
    for i in range(4):
        ins.append(I("affine_select", out=dmask[:, i, :], in_=dmask[:, i, :], pattern=[[1, 512]],
                     compare_op=ALU.is_gt, fill=0.0, base=-128 * i, channel_multiplier=-1))
    op("pool", ins, reads=[B_const], writes=[B_const])
    cast_win(0)

    ar.reset()
    c_sb = ar.t("c_sb", [8, 128], F32)
    c_act = ar.t("c_act", [8, 128], F32)
    fg_row = ar.t("fg_row", [1, 1024], F32)
    B_c = Buf("c_sb"); B_ca = Buf("c_act"); B_fgr = Buf("fg_row")
    s_misc = P.new_sem("misc")
    op("sp", I("dma_start", out=c_sb[:], in_=c_d[:, :]), writes=[B_c], dma_sem=s_misc)
    op("sp", I("dma_start", out=fg_row[:], in_=fg_d[:, :]), writes=[B_fgr], dma_sem=s_misc)
    op("act", I("activation", c_act[:], c_sb[:], AF.Silu), reads=[B_c], writes=[B_ca])
    op("pe", I("transpose", out=bank(0, 8), in_=c_act[:], identity=ident_f[0:8, 0:8]),
       reads=[B_ca, B_const], writes=[BK[0]])
    op("dve", I("tensor_copy", cactT[:], bank(0, 8)), reads=[BK[0]], writes=[B_const])
    op("dve", [I("tensor_copy", cact_bc[:, kc, :], cactT[:, kc:kc + 1].to_broadcast([128, 128])) for kc in range(KC)],
       reads=[B_const], writes=[B_const])
    op("pe", [I("matmul", bank(1), ones_row[0:1, :], fg_row[0:1, 0:512], start=True, stop=True),
              I("matmul", bank(2), ones_row[0:1, :], fg_row[0:1, 512:1024], start=True, stop=True)],
       reads=[B_fgr, B_const], writes=[BK[1], BK[2]])
    op("dve", [I("tensor_copy", fg_b[:, 0:512], bank(1)), I("tensor_copy", fg_b[:, 512:1024], bank(2))],
       reads=[BK[1], BK[2]], writes=[B_const])

    if stop == "p0":
        P.emit()
        return nc
    s_wa = [P.new_sem("wa%d" % i) for i in range(4)]
    s_wo = [P.new_sem("wo0"), P.new_sem("wo1")]
    s_small = P.new_sem("small")
    s_win = P.new_sem("win")
    s_xt = [P.new_sem("xt0"), P.new_sem("xt1")]
    s_qk = [P.new_sem("qk%d" % i) for i in range(4)]
    s_vst = [P.new_sem("vst%d" % i) for i in range(4)]
    s_hd = [P.new_sem("hd0"), P.new_sem("hd1")]
    s_uT = [P.new_sem("uT0"), P.new_sem("uT1")]
    s_x3l = [P.new_sem("x3a"), P.new_sem("x3b")]
    s_x3o = P.new_sem("x3o")
    s_aTt = P.new_sem("aTt")
    s_wup = [P.new_sem("wup%d" % i) for i in range(3)]
    s_wdn = [P.new_sem("wdn%d" % i) for i in range(3)]

    def norm_to_hT(xs_, B_xtile, s, vi_shift, vi_scale, sc, W, part="ab", hT=None, B_hT=None, ci=0):
        xh = W["xhat"][s % 2]
        B_xh = W["B_xhat"][s % 2]
        hT = W["hT"] if hT is None else hT
        B_hT = W["B_hT"] if B_hT is None else B_hT
        B_sc = W["B_sc"][ci]
        if "a" in part:
            op("act", I("activation", W["junk"][:], xs_, AF.Square, accum_out=sc),
               reads=[B_xtile], writes=[W["B_junk"], B_sc])
            op("act", I("activation", sc, sc, AF.Ln, bias=W["eps"][:, 0:1], scale=1.0 / D), reads=[B_sc, W["B_stat"]], writes=[B_sc])
            op("act", I("activation", sc, sc, AF.Exp, scale=-0.5), reads=[B_sc], writes=[B_sc])
            op("dve", I("tensor_scalar", out=xh[:], in0=xs_, scalar1=sc, scalar2=None, op0=ALU.mult),
               reads=[B_xtile, B_sc], writes=[B_xh])
        if "b" in part:
            for half in range(2):
                op("pe", [I("transpose", out=ptb[half][:, k4 * 128:(k4 + 1) * 128],
                            in_=xh[:, (half * 4 + k4) * 128:(half * 4 + k4 + 1) * 128], identity=ident_bf[:])
                          for k4 in range(4)],
                   reads=[B_xh, B_const], writes=[B_pt[half]])
                for k4 in range(4):
                    kc = half * 4 + k4
                    op("dve", I("tensor_scalar", out=hT[:, kc, s * 128:(s + 1) * 128], in0=ptb[half][:, k4 * 128:(k4 + 1) * 128],
                                scalar1=shsc[:, vi_scale, kc:kc + 1], scalar2=shsc[:, vi_shift, kc:kc + 1],
                                op0=ALU.mult, op1=ALU.add),
                       reads=[B_pt[half], B_layer], writes=[B_hT])

    for l in range(DEPTH):
        x_src = x_d if l == 0 else xs_d
        B_xsrc = B_x if l == 0 else B_xs
        last = (l == DEPTH - 1)
        x_dst = out_d if last else xs_d
        B_xdst = B_out if last else B_xs

        P.barrier()
        ar.reset()
        wa = [ar.t("wa", [128, KC, 512], F32) for _ in range(4)]
        B_wa = [Buf("wa%d" % i) for i in range(4)]
        brow = ar.t("brow", [1, 2, 1024], F32)
        b48 = ar.t("b48", [48, 128], F32)
        g16 = ar.t("g16", [8, 128], F32)
        cw1 = ar.t("cw1", [128, 128], F32)
        cw2 = ar.t("cw2", [4, 128], F32)
        cb44 = ar.t("cb44", [44, 128], F32)
        ext_sb = ar.t("ext_sb", [1, 8, 640], F32)
        hank = ar.t("hank", [128, 8, 2, 128], F32)
        wo_st = [ar.t("wo_st", [128, 1024], F32) for _ in range(2)]
        B_wo = [Buf("wo0"), Buf("wo1")]
        B_small = Buf("small")
        B_exts = Buf("ext_sb"); B_hank = Buf("hank")

        op("sp", I("dma_start", out=b48[:], in_=bada_d[l].rearrange("(a b) -> a b", b=128)),
           writes=[B_small], dma_sem=s_small)
        op("sp", I("dma_start", out=brow[0:1, 0, :], in_=bada_d[l:l + 1, 2 * D:3 * D]), writes=[B_small], dma_sem=s_small)
        op("sp", I("dma_start", out=brow[0:1, 1, :], in_=bada_d[l:l + 1, 5 * D:6 * D]), writes=[B_small], dma_sem=s_small)
        op("sp", I("dma_start", out=g16[:], in_=g_d[l].rearrange("(a two) d -> a (two d)", two=2)), writes=[B_small], dma_sem=s_small)
        op("sp", I("dma_start", out=cw1[:], in_=convw_d[l, 0:128, :]), writes=[B_small], dma_sem=s_small)
        op("sp", I("dma_start", out=cw2[:], in_=convw_d[l, 128:132, :]), writes=[B_small], dma_sem=s_small)
        op("sp", I("dma_start", out=cb44[:], in_=convb_d[l]), writes=[B_small], dma_sem=s_small)
        op("sp", I("dma_start", out=ext_sb[0:1, :, 0:256], in_=relb_d[l:l + 1, :, 1:257]), writes=[B_exts], dma_sem=s_small)

        op("pe", [I("transpose", out=bank(0, 48), in_=b48[:], identity=ident_f[0:48, 0:48]),
                  I("transpose", out=pg[:, 512:512 + 8], in_=g16[:], identity=ident_f[0:8, 0:8]),
                  I("transpose", out=bank(2, 128), in_=cw1[:], identity=ident_f[:]),
                  I("transpose", out=pg[:, 2 * 512 + 128:2 * 512 + 132], in_=cw2[:], identity=ident_f[0:4, 0:4]),
                  I("transpose", out=bank(3, 44), in_=cb44[:], identity=ident_f[0:44, 0:44])],
           reads=[B_small, B_const], writes=[BK[0], BK[1], BK[2], BK[3]])
        op("dve", [I("tensor_copy", bT[:], bank(0, 48)),
                   I("tensor_copy", gT[:], pg[:, 512:512 + 8]),
                   I("tensor_copy", cwT[:], bank(2, 132)),
                   I("tensor_copy", cbT[:], bank(3, 44))],
           reads=[BK[0], BK[1], BK[2], BK[3]], writes=[B_layer])
        op("pool", I("memset", carry[:], 0.0), writes=[B_carry[0], B_carry[1]])

        op("dve", I("tensor_copy", ext_sb[0:1, :, 256:640], ext_sb[0:1, :, 255:256].to_broadcast([1, 8, 384])),
           reads=[B_exts], writes=[B_exts])
        op("sp", I("dma_start", out=ext_d[:, :], in_=ext_sb[0:1, :, :]), reads=[B_exts], writes=[B_ext], dma_sem=s_small)
        for hh in range(8):
            for bi, base in ((0, 128), (1, 0)):
                src = bass.AP(ext_d, hh * 640 + base, [[1, 128], [1, 128]])
                op("sp", I("dma_start", out=hank[:, hh, bi, :], in_=src), reads=[B_ext], writes=[B_hank], dma_sem=s_small)
        srcc = bass.AP(ext_d, 300, [[1, 128], [640, 8], [1, 1]])
        op("sp", I("dma_start", out=cconst[:].rearrange("p (a b) -> p a b", b=1), in_=srcc, allow_slow_non_contiguous=True),
           reads=[B_ext], writes=[B_layer], dma_sem=s_small)
        for hh in range(8):
            bk = 4 + (hh % 2)
            op("pe", I("matmul", bank(bk, 256), Jf[:], hank[:, hh, :, :].rearrange("p a b -> p (a b)"), start=True, stop=True),
               reads=[B_hank, B_const], writes=[BK[bk]])
            op("dve", I("tensor_copy", biasT[:, hh, :], bank(bk, 256)), reads=[BK[bk]], writes=[B_layer])
        op("pool", I("memset", biasT[64:128, :, 128:192], NEG), writes=[B_layer])

        for cb in range(12):
            which, half = cb // 2, cb % 2
            sl = cb % 4
            src = wada_d[l, :, cb * 512:(cb + 1) * 512].rearrange("(kc p) c -> p kc c", p=128)
            op("sp", I("dma_start", out=wa[sl][:], in_=src), writes=[B_wa[sl]], dma_sem=s_wa[sl])
            if which in (2, 5):
                gi = 0 if which == 2 else 1
                ins = [I("matmul", bank(5), cact_bc[:, kc, :], wa[sl][:, kc, :], start=(kc == 0), stop=False)
                       for kc in range(KC)]
                ins.append(I("matmul", bank(5), ones_row[0:1, :], brow[0:1, gi, half * 512:(half + 1) * 512],
                             start=False, stop=True))
                op("pe", ins, reads=[B_wa[sl], B_const, B_small], writes=[BK[5]])
                op("dve", I("tensor_copy", gate_b[:, gi, half * 512:(half + 1) * 512], bank(5)),
                   reads=[BK[5]], writes=[B_layer])
            else:
                vi = {0: 0, 1: 1, 3: 2, 4: 3}[which]
                ins = []
                for ch in range(4):
                    for kc in range(KC):
                        ins.append(I("matmul", pg[:, 3 * 512 + ch:3 * 512 + ch + 1], wa[sl][:, kc, ch * 128:(ch + 1) * 128],
                                     cactT[:, kc:kc + 1], start=(kc == 0), stop=(kc == KC - 1)))
                op("pe", ins, reads=[B_wa[sl], B_const], writes=[BK[3]])
                addc = 1.0 if vi in (1, 3) else 0.0
                op("dve", I("scalar_tensor_tensor", out=shsc[:, vi, half * 4:(half + 1) * 4], in0=pg[:, 3 * 512:3 * 512 + 4],
                            scalar=addc, in1=bT[:, cb * 4:(cb + 1) * 4], op0=ALU.add, op1=ALU.add),
                   reads=[BK[3], B_layer], writes=[B_layer])

        for hp in range(8):
            sl = hp % 2
            op("sp", I("dma_start", out=wo_st[sl][:], in_=wout_d[l, hp * 128:(hp + 1) * 128, :]),
               writes=[B_wo[sl]], dma_sem=s_wo[sl])
            op("dve", I("scalar_tensor_tensor", out=wout2[:, hp, :], in0=wo_st[sl][:], scalar=gT[:, hp:hp + 1],
                        in1=gate_b[:, 0, :], op0=ALU.mult, op1=ALU.mult),
               reads=[B_wo[sl], B_layer], writes=[B_wout2])

        if stop == "L0":
            P.emit()
            return nc
        P.barrier()
        ar.reset()
        win_sb = ar.t("win_sb", [128, KC, 3 * D], BF16)
        B_win = Buf("win_sb")
        xt = [ar.t("xt", [128, 4, D], F32) for _ in range(2)]
        B_xt = [Buf("xt0"), Buf("xt1")]
        W1 = dict(junk=ar.t("junk", [128, D], BF16), B_junk=Buf("junk"),
                  xhat=[ar.t("xhat", [128, D], BF16) for _ in range(2)], B_xhat=[Buf("xh0"), Buf("xh1")],
                  hT=ar.t("hT", [128, KC, 512], BF16), B_hT=Buf("hT"),
                  B_stat=Buf("stat"), eps=ar.t("eps1", [128, 1], F32), B_sc=[Buf("sc%d" % i) for i in range(16)])
        stat = ar.t("stat", [128, 16], F32)
        hT = W1["hT"]; B_hT = W1["B_hT"]
        op("pool", I("memset", W1["eps"][:], EPS), writes=[W1["B_stat"]])
        qk_st = [ar.t("qk_st", [128, 512], BF16) for _ in range(4)]
        B_qk = [Buf("qk%d" % i) for i in range(4)]
        v_st = [ar.t("v_st", [128, 8, DH], BF16) for _ in range(4)]
        B_vst = [Buf("vst%d" % i) for i in range(4)]

        for kc in range(KC if not os.environ.get("NOWIN") else 0):
            op("sp", I("dma_start", out=win_sb[:, kc, :], in_=winb_d[l, kc * 128:(kc + 1) * 128, :]),
               reads=[B_winb[l]], writes=[B_win], dma_sem=s_win)

        if l == 0 and not os.environ.get("NOCAST"):
            cast_ffn(0)
            for l2 in range(1, DEPTH):
                cast_win(l2)
                cast_ffn(l2)

        def load_x(T):
            sl = T % 2
            src = x_src[T * 512:(T + 1) * 512, :].rearrange("(s p) d -> p s d", p=128)
            op("sp", I("dma_start", out=xt[sl][:], in_=src), reads=[B_xsrc], writes=[B_xt[sl]], dma_sem=s_xt[sl])

        hT2 = ar.t("hT_b", [128, KC, 512], BF16)
        hTs = [hT, hT2]
        B_hTs = [B_hT, Buf("hT_b")]

        def norm_a(T, s):
            ci = (T % 2) * 4 + s
            norm_to_hT(xt[T % 2][:, s, :], B_xt[T % 2], s, 0, 1, stat[:, ci:ci + 1], W1, part="a", ci=ci)

        def norm_b(T, s):
            ci = (T % 2) * 4 + s
            norm_to_hT(xt[T % 2][:, s, :], B_xt[T % 2], s, 0, 1, stat[:, ci:ci + 1], W1, part="b",
                       hT=hTs[T % 2], B_hT=B_hTs[T % 2], ci=ci)

        load_x(0)
        if NT > 1:
            load_x(1)
        for s in range(4):
            norm_a(0, s)
            norm_b(0, s)
        stn = 0
        for T in range(NT):
            hT = hTs[T % 2]
            B_hT = B_hTs[T % 2]
            if T + 2 < NT:
                load_x(T + 2)
            gi = [0]

            def after_group():
                g_ = gi[0]
                gi[0] += 1
                if T + 1 < NT:
                    if g_ % 6 == 0:
                        norm_a(T + 1, g_ // 6)
                    if g_ % 6 == 4:
                        norm_b(T + 1, g_ // 6)
            P1PART = int(os.environ.get("P1PART", "3"))
            for cbk in range(24 if P1PART >= 2 else 0):
                typ = cbk // 4
                if typ in (2, 5):
                    continue
                bk = cbk % 2
                grp = 0 if typ < 3 else 1
                isq = typ in (0, 3)
                hp = (cbk % 4) + 4 * grp
                op("pe", [I("matmul", bank(bk), win_sb[:, kc, cbk * 128:(cbk + 1) * 128], hT[:, kc, :],
                            start=(kc == 0), stop=(kc == KC - 1)) for kc in range(KC)],
                   reads=[B_win, B_hT], writes=[BK[bk]])
                sl = stn % 4
                stn += 1
                scl = 0.125 if isq else 1.0
                if stn % 2:
                    op("act", I("activation", qk_st[sl][:], bank(bk), AF.Copy, scale=scl), reads=[BK[bk]], writes=[B_qk[sl]])
                else:
                    op("dve", I("tensor_scalar", out=qk_st[sl][:], in0=bank(bk), scalar1=scl, scalar2=None, op0=ALU.mult),
                       reads=[BK[bk]], writes=[B_qk[sl]])
                dstt = qT_d if isq else kT_d
                B_dst = B_qT if isq else B_kT
                op("sp", I("dma_start", out=dstt[hp * 128:(hp + 1) * 128, T * 512:(T + 1) * 512], in_=qk_st[sl][:]),
                   reads=[B_qk[sl]], writes=[B_dst], dma_sem=s_qk[sl])
                after_group()
            for grp in range(2 if P1PART >= 3 else 0):
                c0 = (2 if grp == 0 else 5) * 512
                for s in range(4):
                    bk = 2 + (s % 2)
                    op("pe", [I("matmul", bank(bk), hT[:, kc, s * 128:(s + 1) * 128], win_sb[:, kc, c0:c0 + 512],
                                start=(kc == 0), stop=(kc == KC - 1)) for kc in range(KC)],
                       reads=[B_win, B_hT], writes=[BK[bk]])
                    sl = (grp * 4 + s) % 4
                    op("act", I("activation", v_st[sl][:].rearrange("p a b -> p (a b)"), bank(bk), AF.Copy),
                       reads=[BK[bk]], writes=[B_vst[sl]])
                    op("sp", I("dma_start", out=v_d[grp * 8:(grp + 1) * 8, :, 4 * T + s, :].rearrange("h p d -> p h d"),
                               in_=v_st[sl][:]),
                       reads=[B_vst[sl]], writes=[B_v], dma_sem=s_vst[sl])
                    after_group()

        if stop == "p1":
            P.emit()
            return nc
        P.barrier()
        ar.reset()
        qTh = [ar.t("qTh", [64, S], BF16) for _ in range(2)]
        kTh = [ar.t("kTh", [64, S], BF16) for _ in range(2)]
        vh = [ar.t("vh", [128, NB, 65], BF16) for _ in range(2)]
        B_qh = [Buf("qh0"), Buf("qh1")]; B_kh = [Buf("kh0"), Buf("kh1")]; B_vh = [Buf("vh0"), Buf("vh1")]
        uT = [ar.t("uT", [64, S], BF16) for _ in range(2)]
        B_uT = [Buf("uT0"), Buf("uT1")]
        tmpb = [ar.t("tmpb", [128, 256], F32) for _ in range(2)]
        B_tmpb = [Buf("tmpb0"), Buf("tmpb1")]
        PT = [ar.t("PT", [128, 5, 128], BF16) for _ in range(3)]
        B_PT = [Buf("PT%d" % i) for i in range(3)]
        rcp = ar.t("rcp", [128, 4], F32)
        B_rcp = Buf("rcp")
        ubf = [ar.t("ubf", [128, 64], BF16) for _ in range(2)]
        B_ubf = [Buf("ubf0"), Buf("ubf1")]
        e_sb = [ar.t("e_sb", [128, 1024], F32) for _ in range(3)]
        B_e = [Buf("e%d" % i) for i in range(3)]
        sp_sb = [ar.t("sp_sb", [128, 1024], BF16) for _ in range(4)]
        B_sp = [Buf("sp%d" % i) for i in range(4)]
        acc = ar.t("acc", [128, 512], BF16)
        B_acc = Buf("acc")
        W_sb = [ar.t("W_sb", [128, 1024], BF16) for _ in range(4)]
        B_W = [Buf("W%d" % i) for i in range(4)]
        one1 = ar.t("one1", [128, 1], F32)
        B_one1 = Buf("one1")
        op("pool", I("memset", one1[:], 1.0), writes=[B_one1])
        op("pool", I("memset", vh[0][:, :, 64:65], 1.0), writes=[B_vh[0]])
        op("pool", I("memset", vh[1][:, :, 64:65], 1.0), writes=[B_vh[1]])

        def load_head(hd):
            sl = hd % 2
            op("sp", I("dma_start", out=qTh[sl][:], in_=qT_d[hd * 64:(hd + 1) * 64, :]),
               reads=[B_qT], writes=[B_qh[sl]], dma_sem=s_hd[sl])
            op("sp", I("dma_start", out=kTh[sl][:], in_=kT_d[hd * 64:(hd + 1) * 64, :]),
               reads=[B_kT], writes=[B_kh[sl]], dma_sem=s_hd[sl])
            op("sp", I("dma_start", out=vh[sl][:, :, 0:64], in_=v_d[hd]),
               reads=[B_v], writes=[B_vh[sl]], dma_sem=s_hd[sl])

        load_head(0)
        NHA = min(8, int(os.environ.get("NHMAX", NH)))
        citems = [(hd, p) for hd in range(NHA) for p in range(NB)]
        NCI = len(citems)

        def hvs(hd):
            hs = hd % 2
            return qTh[hs], kTh[hs], vh[hs], B_qh[hs], B_kh[hs], B_vh[hs], uT[hs], B_uT[hs]

        def jvalid(p):
            return [j for j in range(5) if p - 4 + j >= 0]

        def C0(n):
            hd, p = citems[n]
            q_, k_, v_, Bq, Bk_, Bv, u_, Bu = hvs(hd)
            ps2 = 2 * (n % 2)
            op("pe", [I("matmul", pg[:, ps2 * 512 + j * 128: ps2 * 512 + (j + 1) * 128],
                        k_[:, (p - 4 + j) * 128:(p - 3 + j) * 128], q_[:, p * 128:(p + 1) * 128],
                        start=True, stop=True) for j in jvalid(p)],
               reads=[Bq, Bk_], writes=[BK[ps2], BK[ps2 + 1]])

        def C1(n):
            hd, p = citems[n]
            ps2 = 2 * (n % 2)
            jv = jvalid(p)
            j0 = jv[0]
            pti = n % 3
            nlow = len([j for j in jv if j < 3])
            if nlow > 0:
                op("act", I("activation", PT[pti][:, j0:j0 + nlow, :].rearrange("p a b -> p (a b)"),
                            pg[:, ps2 * 512 + j0 * 128: ps2 * 512 + (j0 + nlow) * 128], AF.Exp,
                            bias=cconst[:, hd:hd + 1]),
                   reads=[BK[ps2], B_layer], writes=[B_PT[pti]])
            jh0 = max(3, j0)
            nh_ = 5 - jh0
            tb = n % 2
            op("dve", I("tensor_tensor", out=tmpb[tb][:, 0:nh_ * 128],
                        in0=pg[:, ps2 * 512 + jh0 * 128: ps2 * 512 + 640],
                        in1=biasT[:, hd, (jh0 - 3) * 128:256], op=ALU.add),
               reads=[BK[ps2], BK[ps2 + 1], B_layer], writes=[B_tmpb[tb]])

        def C2(n):
            hd, p = citems[n]
            jv = jvalid(p)
            j0 = jv[0]
            pti = n % 3
            jh0 = max(3, j0)
            nh_ = 5 - jh0
            tb = n % 2
            op("act", I("activation", PT[pti][:, jh0:5, :].rearrange("p a b -> p (a b)"),
                        tmpb[tb][:, 0:nh_ * 128], AF.Exp),
               reads=[B_tmpb[tb]], writes=[B_PT[pti]])
            if j0 == 0:
                op("pool", I("memset", PT[pti][0:64, 0, 64:128], 0.0), reads=[B_PT[pti]], writes=[B_PT[pti]])

        def C3(n):
            hd, p = citems[n]
            q_, k_, v_, Bq, Bk_, Bv, u_, Bu = hvs(hd)
            jv = jvalid(p)
            pti = n % 3
            ob = 4 + (n % 2)
            op("pe", [I("matmul", bank(ob, 65), PT[pti][:, j, :], v_[:, p - 4 + j, :],
                        start=(j == jv[0]), stop=(j == jv[-1])) for j in jv],
               reads=[B_PT[pti], Bv], writes=[BK[ob]])

        def C4(n):
            ob = 4 + (n % 2)
            rc = rcp[:, (n % 4):(n % 4) + 1]
            op("dve", I("reciprocal", rc, pg[:, ob * 512 + 64: ob * 512 + 65]), reads=[BK[ob]], writes=[B_rcp])
            op("dve", I("tensor_scalar", out=ubf[n % 2][:], in0=bank(ob, 64), scalar1=rc, scalar2=None, op0=ALU.mult),
               reads=[BK[ob], B_rcp], writes=[B_ubf[n % 2]])

        def C5(n):
            us = n % 2
            op("pe", I("transpose", out=ptb[us][0:64, 0:128], in_=ubf[us][:], identity=ident_bf[:]),
               reads=[B_ubf[us], B_const], writes=[B_pt[us]])

        def C6(n):
            hd, p = citems[n]
            q_, k_, v_, Bq, Bk_, Bv, u_, Bu = hvs(hd)
            us = n % 2
            op("dve", I("tensor_copy", u_[:, p * 128:(p + 1) * 128], ptb[us][0:64, 0:128]),
               reads=[B_pt[us]], writes=[Bu])
            if p == NB - 1:
                op("sp", I("dma_start", out=aT_d[hd], in_=u_[:]), reads=[Bu], writes=[B_aT], dma_sem=s_uT[hd % 2])
            if p == 0 and hd + 1 < NH:
                load_head(hd + 1)

        for t in range(NCI + 6):
            if t < NCI:
                C0(t)
            if 0 <= t - 3 < NCI:
                C3(t - 3)
            if 0 <= t - 5 < NCI:
                C5(t - 5)
            if 0 <= t - 1 < NCI:
                C1(t - 1)
            if 0 <= t - 2 < NCI:
                C2(t - 2)
            if 0 <= t - 4 < NCI:
                C4(t - 4)
            if 0 <= t - 6 < NCI:
                C6(t - 6)

        acc3 = [acc] + [ar.t("acc_x", [128, 512], BF16) for _ in range(2)]
        B_acc3 = [B_acc, Buf("acc1"), Buf("acc2")]
        dmaskR = ar.t("dmaskR", [128, 4, 512], BF16)
        B_dmR = Buf("dmaskR")
        op("pool", [I("tensor_copy", dmaskR[:, r, :], dmask[:, 3 - r, :]) for r in range(4)], reads=[B_const], writes=[B_dmR])
        pso = [ptb[0].bitcast(F32), ptb[1].bitcast(F32)]
        items = []
        for hd in range(8, int(os.environ.get("NHMAX", NH))):
            for T in range(NT):
                npair = 2 * T + 2
                for i in range(npair):
                    items.append((hd, T, i, npair))
        NI = len(items)

        def hv(hd):
            hs = hd % 2
            return qTh[hs], kTh[hs], vh[hs], B_qh[hs], B_kh[hs], B_vh[hs], uT[hs], B_uT[hs]

        def zpair(n):
            b3 = n % 3
            return pg[:, b3 * 1024:(b3 + 1) * 1024], [BK[2 * b3], BK[2 * b3 + 1]]

        def kbs_of(n):
            hd, T, i, npair = items[n]
            khi = 2 * npair - 1 - 2 * i
            return khi, khi - 1

        def S0(n):
            hd, T, i, npair = items[n]
            q_, k_, v_, Bq, Bk_, Bv, u_, Bu = hv(hd)
            zp, Bz = zpair(n)
            khi, klo = kbs_of(n)
            qs = q_[:, T * 512:(T + 1) * 512]
            op("pe", [I("matmul", zp[:, 0:512], k_[:, khi * 128:(khi + 1) * 128], qs, start=True, stop=True),
                      I("matmul", zp[:, 512:1024], k_[:, klo * 128:(klo + 1) * 128], qs, start=True, stop=True)],
               reads=[Bq, Bk_], writes=Bz)

        def S1(n):
            zp, Bz = zpair(n)
            op("act", I("activation", e_sb[n % 3][:], zp, AF.Exp), reads=Bz, writes=[B_e[n % 3]])

        def S2(n):
            hd, T, i, npair = items[n]
            khi, klo = kbs_of(n)
            sp_ = sp_sb[n % 4]
            op("act", I("activation", sp_[:], e_sb[n % 3][:], AF.Ln, bias=one1[:, 0:1]),
               reads=[B_e[n % 3], B_one1], writes=[B_sp[n % 4]])
            if khi >= 4 * T:
                r0 = 3 - (khi - 4 * T)
                op("dve", I("tensor_tensor", out=sp_[:], in0=sp_[:], in1=dmaskR[:, r0:r0 + 2, :].rearrange("p a b -> p (a b)"),
                            op=ALU.mult),
                   reads=[B_sp[n % 4], B_dmR], writes=[B_sp[n % 4]])
            if i + 1 < npair:
                if i == 0:
                    op("pool", I("tensor_tensor", out=acc3[(n + 1) % 3][:], in0=sp_[:, 0:512], in1=sp_[:, 512:1024], op=ALU.add),
                       reads=[B_sp[n % 4]], writes=[B_acc3[(n + 1) % 3]])
                else:
                    op("pool", I("tensor_tensor", out=acc3[(n + 1) % 3][:], in0=acc3[n % 3][:], in1=sp_[:, 0:512], op=ALU.add),
                       reads=[B_acc3[n % 3], B_sp[n % 4]], writes=[B_acc3[(n + 1) % 3]])
                    op("pool", I("tensor_tensor", out=acc3[(n + 1) % 3][:], in0=acc3[(n + 1) % 3][:], in1=sp_[:, 512:1024], op=ALU.add),
                       reads=[B_acc3[(n + 1) % 3], B_sp[n % 4]], writes=[B_acc3[(n + 1) % 3]])

        def S3(n):
            hd, T, i, npair = items[n]
            zp, Bz = zpair(n)
            sp_ = sp_sb[n % 4]
            hi, lo = zp[:, 0:512], zp[:, 512:1024]
            ins = [I("matmul", hi, triN[:], sp_[:, 0:512], start=False, stop=(i == 0), skip_group_check=True)]
            if i > 0:
                ins.append(I("matmul", hi, onesN[:], acc3[n % 3][:], start=False, stop=True, skip_group_check=True))
            ins += [I("matmul", lo, triN[:], sp_[:, 512:1024], start=False, stop=False, skip_group_check=True),
                    I("matmul", lo, onesN[:], sp_[:, 0:512], start=False, stop=(i == 0), skip_group_check=True)]
            if i > 0:
                ins.append(I("matmul", lo, onesN[:], acc3[n % 3][:], start=False, stop=True, skip_group_check=True))
            rd = [B_sp[n % 4], B_const] + ([B_acc3[n % 3]] if i > 0 else [])
            op("pe", ins, reads=rd, writes=Bz)

        def S4(n):
            hd, T, i, npair = items[n]
            khi, klo = kbs_of(n)
            w_ = W_sb[n % 4]
            zp, Bz = zpair(n)
            op("act", I("activation", w_[:], zp, AF.Exp), reads=Bz, writes=[B_W[n % 4]])
            if khi >= 4 * T:
                r0 = 3 - (khi - 4 * T)
                op("dve", I("tensor_tensor", out=w_[:], in0=w_[:], in1=dmaskR[:, r0:r0 + 2, :].rearrange("p a b -> p (a b)"),
                            op=ALU.mult),
                   reads=[B_W[n % 4], B_dmR], writes=[B_W[n % 4]])

        def S5(n):
            hd, T, i, npair = items[n]
            q_, k_, v_, Bq, Bk_, Bv, u_, Bu = hv(hd)
            khi, klo = kbs_of(n)
            g = (hd - 8) * NT + T
            po = pso[g % 2]
            w_ = W_sb[n % 4]
            op("pe", [I("matmul", po[0:64, 0:512], v_[:, khi, 0:64], w_[:, 0:512], start=(i == 0), stop=False),
                      I("matmul", po[0:64, 0:512], v_[:, klo, 0:64], w_[:, 512:1024], start=False, stop=(i == npair - 1))],
               reads=[B_W[n % 4], Bv], writes=[B_pt[g % 2]])
            if i == npair - 1:
                op("dve", I("tensor_copy", u_[:, T * 512:(T + 1) * 512], po[0:64, 0:512]),
                   reads=[B_pt[g % 2]], writes=[Bu])
                if T == NT - 1:
                    op("sp", I("dma_start", out=aT_d[hd], in_=u_[:]), reads=[Bu], writes=[B_aT], dma_sem=s_uT[hd % 2])

        for t in range(NI + 4):
            if t < NI:
                S0(t)
            if 0 <= t - 1 < NI:
                S2(t - 1)
            if t < NI:
                S1(t)
            if 0 <= t - 2 < NI:
                S3(t - 2)
                S4(t - 2)
            if 0 <= t - 3 < NI:
                S5(t - 3)
                hd5, T5, i5, _ = items[t - 3]
                if T5 == 0 and i5 == 0 and hd5 + 1 < NH:
                    load_head(hd5 + 1)

        if stop == "p2":
            P.emit()
            return nc
        P.barrier()
        ar.reset()
        x3s = [ar.t("x3", [128, 4, D], F32) for _ in range(2)]
        B_x3s = [Buf("x3a"), Buf("x3b")]
        aTt = ar.t("aTt", [128, 8, 512], BF16)
        B_aTt = Buf("aTt")
        W3 = dict(junk=ar.t("junk3", [128, D], BF16), B_junk=Buf("junk3"),
                  xhat=[ar.t("xhat3", [128, D], BF16) for _ in range(2)], B_xhat=[Buf("xh3_0"), Buf("xh3_1")],
                  hT=ar.t("hT3", [128, KC, 512], BF16), B_hT=Buf("hT3"),
                  B_stat=Buf("stat3"), eps=ar.t("eps3", [128, 1], F32), B_sc=[Buf("sc3_%d" % i) for i in range(16)])
        stat = ar.t("stat3", [128, 16], F32)
        hT = W3["hT"]; B_hT = W3["B_hT"]; B_stat = W3["B_stat"]; junk = W3["junk"]; B_junk = W3["B_junk"]
        eps3 = W3["eps"]
        op("pool", I("memset", eps3[:], EPS), writes=[B_stat])
        sq = [ar.t("sq", [128, 512], F32) for _ in range(2)]
        B_sq = [Buf("sq0"), Buf("sq1")]
        rstd2 = ar.t("rstd2", [128, 4, 2], F32)
        B_rstd2 = Buf("rstd2")
        wup_sb = [ar.t("wup_sb", [128, KC, 2, 128], BF16) for _ in range(3)]
        B_wup = [Buf("wup%d" % i) for i in range(3)]
        wdn_sb = [ar.t("wdn_sb", [128, 4, 512], BF16) for _ in range(3)]
        B_wdn = [Buf("wdn%d" % i) for i in range(3)]
        aF = ar.t("aF", [128, NJ, 512], BF16)
        B_aF = Buf("aF")
        y0 = [ar.t("y0", [128, 512], F32) for _ in range(6)]
        B_y0 = [Buf("y0_%d" % i) for i in range(6)]
        addv = ar.t("addv", [128, 44, 2], F32)
        tAB = ar.t("tAB", [128, 2, 44], F32)
        B_addv = Buf("addv")
        sg = [ar.t("sg", [128, 512], F32) for _ in range(2)]
        B_sg = [Buf("sg0"), Buf("sg1")]
        tmp3 = [ar.t("tmp3", [128, 512], F32) for _ in range(2)]
        B_tmp3 = [Buf("tmp3_0"), Buf("tmp3_1")]

        wupc = 0
        wdnc = 0
        def load_x3(T):
            src = x_src[T * 512:(T + 1) * 512, :].rearrange("(s p) d -> p s d", p=128)
            op("sp", I("dma_start", out=x3s[T % 2][:], in_=src), reads=[B_xsrc], writes=[B_x3s[T % 2]], dma_sem=s_x3l[T % 2])

        def load_aTt(T):
            for two in range(2):
                srca = aT_d[:, :, T * 512:(T + 1) * 512].rearrange("(hp two) d t -> two d hp t", two=2)[two]
                op("sp", I("dma_start", out=aTt[two * 64:(two + 1) * 64, :, :], in_=srca),
                   reads=[B_aT], writes=[B_aTt], dma_sem=s_aTt)

        load_x3(0)
        load_aTt(0)
        for T in range(NT):
            x3 = x3s[T % 2]
            B_x3 = B_x3s[T % 2]
            if T + 1 < NT:
                load_x3(T + 1)
            for g in range(2):
                for hh in range(4):
                    hp = g * 4 + hh
                    sl = hp % 2
                    op("act", I("activation", sq[sl][:], aTt[:, hp, :], AF.Square), reads=[B_aTt], writes=[B_sq[sl]])
                    op("pe", [I("matmul", pg[:, s * 512 + g: s * 512 + g + 1], sq[sl][:, s * 128:(s + 1) * 128],
                                ones_col[:, :], start=(hh == 0), stop=(hh == 3)) for s in range(4)],
                       reads=[B_sq[sl], B_const], writes=[BK[0], BK[1], BK[2], BK[3]])
            op("act", [I("activation", rstd2[:, s, :], pg[:, s * 512: s * 512 + 2], AF.Ln, bias=eps3[:, 0:1], scale=1.0 / 512)
                       for s in range(4)],
               reads=[BK[0], BK[1], BK[2], BK[3], B_stat], writes=[B_rstd2])
            op("act", I("activation", rstd2[:].rearrange("p a b -> p (a b)"), rstd2[:].rearrange("p a b -> p (a b)"),
                        AF.Exp, scale=-0.5), reads=[B_rstd2], writes=[B_rstd2])
            for s in range(4):
                for n in range(2):
                    bA = 2 * ((s * 2 + n) % 2)
                    bB = bA + 1
                    ins = [I("matmul", bank(bA), aTt[:, hp, s * 128:(s + 1) * 128], wout2[:, hp, n * 512:(n + 1) * 512],
                             start=(hp == 0), stop=(hp == 3)) for hp in range(4)]
                    ins += [I("matmul", bank(bB), aTt[:, hp, s * 128:(s + 1) * 128], wout2[:, hp, n * 512:(n + 1) * 512],
                              start=(hp == 4), stop=(hp == 7)) for hp in range(4, 8)]
                    op("pe", ins, reads=[B_aTt, B_wout2], writes=[BK[bA], BK[bB]])
                    xv = x3[:, s, n * 512:(n + 1) * 512]
                    op("dve", I("scalar_tensor_tensor", out=xv, in0=bank(bA), scalar=rstd2[:, s, 0:1], in1=xv,
                                op0=ALU.mult, op1=ALU.add), reads=[BK[bA], B_rstd2, B_x3], writes=[B_x3])
                    op("dve", I("scalar_tensor_tensor", out=xv, in0=bank(bB), scalar=rstd2[:, s, 1:2], in1=xv,
                                op0=ALU.mult, op1=ALU.add), reads=[BK[bB], B_rstd2, B_x3], writes=[B_x3])
            if T + 1 < NT:
                load_aTt(T + 1)
            for s in range(4):
                norm_to_hT(x3[:, s, :], B_x3, s, 2, 3, stat[:, s:s + 1], W3, ci=s)
            cold = carry[:, (T + 1) % 2, :, :]
            cnew = carry[:, T % 2, :, :]
            Bcold = B_carry[(T + 1) % 2]
            Bcnew = B_carry[T % 2]
            op("dve", [I("tensor_tensor", out=tAB[:, 0, :], in0=cold[:, :, 0], in1=cwT[:, 0:44], op=ALU.mult),
                       I("tensor_tensor", out=tAB[:, 1, :], in0=cold[:, :, 1], in1=cwT[:, 44:88], op=ALU.mult),
                       I("tensor_tensor", out=addv[:, :, 1], in0=cold[:, :, 1], in1=cwT[:, 0:44], op=ALU.mult)],
               reads=[Bcold, B_layer], writes=[B_addv])
            op("dve", I("tensor_tensor", out=addv[:, :, 0], in0=tAB[:, 0, :], in1=tAB[:, 1, :], op=ALU.add),
               reads=[B_addv], writes=[B_addv])
            def ffn_fin(jj):
                u0 = (2 * jj) % 6
                u1 = (2 * jj + 1) % 6
                op("act", I("activation", sg[jj % 2][:], y0[u0][:], AF.Silu), reads=[B_y0[u0]], writes=[B_sg[jj % 2]])
                op("pool", I("tensor_tensor", out=aF[:, jj, :], in0=y0[u1][:], in1=sg[jj % 2][:], op=ALU.mult),
                   reads=[B_y0[u1], B_sg[jj % 2]], writes=[B_aF])

            for j in range(NJ):
                wsl = wupc % 3
                wupc += 1
                if not os.environ.get("NOWLOAD"):
                    op("sp", I("dma_start", out=wup_sb[wsl][:], in_=wupb_d[l, j]),
                       reads=[B_wupb[l]], writes=[B_wup[wsl]], dma_sem=s_wup[wsl])
                for g in range(2):
                    bk = 2 * (j % 3) + g
                    ui = (2 * j + g) % 6
                    jg = g * NJ + j
                    op("pe", [I("matmul", bank(bk), wup_sb[wsl][:, kc, g, :], hT[:, kc, :], start=(kc == 0), stop=(kc == KC - 1))
                              for kc in range(KC)],
                       reads=[B_wup[wsl], B_hT], writes=[BK[bk]])
                    filler()
                    op("act", I("activation", y0[ui][:], bank(bk), AF.Identity, bias=cbT[:, jg:jg + 1],
                                scale=cwT[:, 88 + jg:88 + jg + 1]),
                       reads=[BK[bk], B_layer], writes=[B_y0[ui]])
                    op("act", I("activation", cnew[:, jg, :], pg[:, bk * 512 + 510: bk * 512 + 512], AF.Copy),
                       reads=[BK[bk]], writes=[Bcnew])
                    op("dve", I("scalar_tensor_tensor", out=y0[ui][:, 1:512], in0=pg[:, bk * 512: bk * 512 + 511],
                                scalar=cwT[:, 44 + jg:44 + jg + 1], in1=y0[ui][:, 1:512], op0=ALU.mult, op1=ALU.add),
                       reads=[BK[bk], B_y0[ui], B_layer], writes=[B_y0[ui]])
                    op("dve", I("scalar_tensor_tensor", out=y0[ui][:, 2:512], in0=pg[:, bk * 512: bk * 512 + 510],
                                scalar=cwT[:, jg:jg + 1], in1=y0[ui][:, 2:512], op0=ALU.mult, op1=ALU.add),
                       reads=[BK[bk], B_y0[ui], B_layer], writes=[B_y0[ui]])
                    op("pool", I("tensor_tensor", out=y0[ui][:, 0:2], in0=y0[ui][:, 0:2], in1=addv[:, jg, :], op=ALU.add),
                       reads=[B_addv, B_y0[ui]], writes=[B_y0[ui]])
                    if g == 1 and j >= 1:
                        ffn_fin(j - 1)
            ffn_fin(NJ - 1)
            for n in range(2):
                for jg0 in range(0, NJ, 4):
                    nj = min(4, NJ - jg0)
                    dsl = wdnc % 3
                    wdnc += 1
                    src = wdnb_d[l, jg0 * 128:(jg0 + nj) * 128, n * 512:(n + 1) * 512].rearrange("(j p) c -> p j c", p=128)
                    if not os.environ.get("NOWLOAD"):
                        op("sp", I("dma_start", out=wdn_sb[dsl][:, 0:nj, :], in_=src),
                           reads=[B_wdnb[l]], writes=[B_wdn[dsl]], dma_sem=s_wdn[dsl])
                    ins = []
                    for s in range(4):
                        for jj in range(nj):
                            j = jg0 + jj
                            ins.append(I("matmul", bank(2 + s), aF[:, j, s * 128:(s + 1) * 128], wdn_sb[dsl][:, jj, :],
                                         start=(j == 0), stop=(j == NJ - 1)))
                    op("pe", ins, reads=[B_aF, B_wdn[dsl]], writes=[BK[2], BK[3], BK[4], BK[5]])
                    filler()
                for s in range(4):
                    ts = s % 2
                    xv = x3[:, s, n * 512:(n + 1) * 512]
                    op("dve", I("tensor_tensor", out=tmp3[ts][:], in0=bank(2 + s), in1=gate_b[:, 1, n * 512:(n + 1) * 512],
                                op=ALU.mult),
                       reads=[BK[2 + s], B_layer], writes=[B_tmp3[ts]])
                    op("pool", I("tensor_tensor", out=xv, in0=xv, in1=tmp3[ts][:], op=ALU.add),
                       reads=[B_tmp3[ts], B_x3], writes=[B_x3])
            if last:
                for s in range(4):
                    sc = stat[:, 8 + s:9 + s]
                    op("act", I("activation", junk[:], x3[:, s, :], AF.Square, accum_out=sc),
                       reads=[B_x3], writes=[B_junk, B_stat])
                    op("act", I("activation", sc, sc, AF.Ln, bias=eps3[:, 0:1], scale=1.0 / D), reads=[B_stat], writes=[B_stat])
                    op("act", I("activation", sc, sc, AF.Exp, scale=-0.5), reads=[B_stat], writes=[B_stat])
                    op("dve", I("scalar_tensor_tensor", out=x3[:, s, :], in0=x3[:, s, :], scalar=sc, in1=fg_b[:],
                                op0=ALU.mult, op1=ALU.mult),
                       reads=[B_x3, B_stat, B_const], writes=[B_x3])
            dst = x_dst[T * 512:(T + 1) * 512, :].rearrange("(s p) d -> p s d", p=128)
            op("sp", I("dma_start", out=dst, in_=x3[:]), reads=[B_x3], writes=[B_xdst], dma_sem=s_x3o)

    P.emit()
    return nc


_CACHE = {}


def make_in_maps(inputs, S, DEPTH, ncores):
    f = lambda a: np.ascontiguousarray(np.asarray(a, dtype=np.float32))
    x = f(inputs["x"]); c = f(inputs["c"])
    g = np.concatenate([f(inputs["g_a"]).reshape(DEPTH, 8, 64), f(inputs["g_b"]).reshape(DEPTH, 8, 64)], axis=1)
    shared = dict(
        w_ada=f(inputs["w_ada"]), b_ada=f(inputs["b_ada"]), w_in=f(inputs["w_in"]),
        rel_bias=f(inputs["rel_bias"]), g=np.ascontiguousarray(g), w_out=f(inputs["w_out"]),
        w_up=f(inputs["w_up"]), conv_w=f(inputs["conv_w"]).reshape(DEPTH, 132, 128),
        conv_b=f(inputs["conv_b"]).reshape(DEPTH, 44, 128), w_down=f(inputs["w_down"]),
        final_g=f(inputs["final_g"]).reshape(1, D),
    )
    maps = []
    for b in range(ncores):
        m = dict(shared)
        m["x"] = np.ascontiguousarray(x[b])
        m["c"] = np.ascontiguousarray(c[b].reshape(8, 128))
        maps.append(m)
    return maps


def kernel(x, c, w_ada, b_ada, w_in, rel_bias, g_a, g_b, w_out, w_up, conv_w, conv_b, w_down, final_g):
    inputs = dict(x=x, c=c, w_ada=w_ada, b_ada=b_ada, w_in=w_in, rel_bias=rel_bias, g_a=g_a, g_b=g_b,
                  w_out=w_out, w_up=w_up, conv_w=conv_w, conv_b=conv_b, w_down=w_down, final_g=final_g)
    Bn, S, _ = np.asarray(x).shape
    DEPTH = np.asarray(w_in).shape[0]
    key = (S, DEPTH)
    if key not in _CACHE:
        _CACHE[key] = build(S, DEPTH)
    nc = _CACHE[key]
    maps = make_in_maps(inputs, S, DEPTH, Bn)
    res = run_bass_kernel_spmd(nc, maps, core_ids=list(range(Bn)))
    out = np.stack([np.asarray(r["out"]) for r in res.results], axis=0)
    return out.astype(np.float32)
```
